# Optimizing a Trainium2 kernel written in Bass

```python
import math
import jax, jax.numpy as jnp
from jax import lax
import numpy as np

D_MODEL = 1024
BATCH = 16
SEQ = 2048
DEPTH = 2

N_MIXERS = 2
MEM_LEN = 256
MEM_HEADS = 4
MEM_HEAD_DIM = D_MODEL // 16
MEM_WIDTH = MEM_HEADS * MEM_HEAD_DIM
RET_HEAD_DIM = D_MODEL // 8
RET_HEADS = (D_MODEL - MEM_WIDTH) // RET_HEAD_DIM
RET_WIDTH = RET_HEADS * RET_HEAD_DIM
RET_CHUNK = 128
ROPE_BASE = 10000.0
FOX_HEAD_DIM = D_MODEL // 16
FOX_HEADS = (D_MODEL - MEM_WIDTH) // FOX_HEAD_DIM
FOX_WIDTH = FOX_HEADS * FOX_HEAD_DIM
FOX_BLOCK = 128
D_FF = 4 * D_MODEL
EPS = 1e-6
N_RET = (DEPTH + N_MIXERS - 1) // N_MIXERS
N_FOX = DEPTH // N_MIXERS
RET_IN = 4 * RET_WIDTH + MEM_WIDTH
FOX_IN = 3 * FOX_WIDTH + FOX_HEADS + MEM_WIDTH

kernel_name = "hybrid_retention_fox_memory_block"


def rmsnorm(x, g):
    xf = x.astype(jnp.float32)
    y = xf * lax.rsqrt(jnp.mean(xf * xf, axis=-1, keepdims=True) + EPS)
    return (y * g.astype(jnp.float32)).astype(x.dtype)


def rotary(x, positions):
    d = x.shape[-1]
    inv_freq = ROPE_BASE ** (-jnp.arange(0, d, 2, dtype=jnp.float32) / d)
    ang = positions.astype(jnp.float32)[..., None] * inv_freq
    cos = jnp.cos(ang)[:, :, None, :]
    sin = jnp.sin(ang)[:, :, None, :]
    xf = x.astype(jnp.float32)
    x1, x2 = xf[..., : d // 2], xf[..., d // 2:]
    return jnp.concatenate([x1 * cos - x2 * sin, x1 * sin + x2 * cos], axis=-1)


def retention_chunkwise(q, k, v):
    B, S, H, d = q.shape
    C = RET_CHUNK
    n = S // C
    log_gamma = jnp.log(1.0 - 2.0 ** (-5.0 - jnp.arange(H, dtype=jnp.float32)))
    idx = jnp.arange(C, dtype=jnp.float32)
    rel = idx[:, None] - idx[None, :]
    decay_mask = jnp.where(rel[None] >= 0, jnp.exp(jnp.maximum(rel, 0.0)[None] * log_gamma[:, None, None]), 0.0)
    q_decay = jnp.exp((idx[None, :] + 1.0) * log_gamma[:, None])
    k_decay = jnp.exp((C - 1.0 - idx[None, :]) * log_gamma[:, None])
    chunk_decay = jnp.exp(C * log_gamma)
    k = k * (d ** -0.5)

    def to_chunks(t):
        return t.reshape(B, n, C, H, t.shape[-1]).transpose(1, 0, 3, 2, 4)

    def step(state, inp):
        qc, kc, vc = inp
        inner = jnp.einsum('bhid,bhjd->bhij', qc, kc) * decay_mask[None]
        o_inner = jnp.einsum('bhij,bhje->bhie', inner, vc)
        o_cross = jnp.einsum('bhid,bhde->bhie', qc * q_decay[None, :, :, None], state)
        state = state * chunk_decay[None, :, None, None] + jnp.einsum('bhjd,bhje->bhde', kc * k_decay[None, :, :, None], vc)
        return state, o_inner + o_cross

    state0 = jnp.zeros((B, H, d, v.shape[-1]), jnp.float32)
    _, out = lax.scan(step, state0, (to_chunks(q), to_chunks(k), to_chunks(v)))
    return out.transpose(1, 0, 3, 2, 4).reshape(B, S, H, v.shape[-1])


def forgetting_attention(q, k, v, log_f):
    B, S, H, d = q.shape
    nb = S // FOX_BLOCK
    c = jnp.cumsum(log_f, axis=1).transpose(0, 2, 1)
    qh = q.transpose(0, 2, 1, 3)
    kh = k.transpose(0, 2, 1, 3)
    vh = v.transpose(0, 2, 1, 3)
    q_blocks = qh.reshape(B, H, nb, FOX_BLOCK, d).transpose(2, 0, 1, 3, 4)
    c_blocks = c.reshape(B, H, nb, FOX_BLOCK).transpose(2, 0, 1, 3)
    starts = jnp.arange(nb, dtype=jnp.int32) * FOX_BLOCK
    key_pos = jnp.arange(S, dtype=jnp.int32)
    scale = d ** -0.5

    def one_block(inp):
        qb, cb, start = inp
        s = jnp.einsum('bhqd,bhkd->bhqk', qb, kh).astype(jnp.float32) * scale
        s = s + cb[..., None] - c[:, :, None, :]
        q_pos = start + jnp.arange(FOX_BLOCK, dtype=jnp.int32)
        causal = key_pos[None, :] <= q_pos[:, None]
        s = jnp.where(causal[None, None], s, -jnp.inf)
        p = jax.nn.softmax(s, axis=-1)
        return jnp.einsum('bhqk,bhkd->bhqd', p.astype(vh.dtype), vh)

    out = lax.map(one_block, (q_blocks, c_blocks, starts))
    return out.transpose(1, 0, 3, 2, 4).reshape(B, S, H, d)


def memory_attention(mq, mem, g_mem, w_mem_kv):
    B, M, _ = mem.shape
    kv = rmsnorm(mem, g_mem) @ w_mem_kv
    mk, mv = jnp.split(kv, 2, axis=-1)
    mk = mk.reshape(B, M, MEM_HEADS, MEM_HEAD_DIM)
    mv = mv.reshape(B, M, MEM_HEADS, MEM_HEAD_DIM)
    s = jnp.einsum('bshd,bmhd->bhsm', mq, mk).astype(jnp.float32) * (MEM_HEAD_DIM ** -0.5)
    p = jax.nn.softmax(s, axis=-1)
    return jnp.einsum('bhsm,bmhd->bshd', p.astype(mv.dtype), mv)


def setup_inputs(seed: int = 0) -> dict:
    key = jax.random.key(seed)
    ks = jax.random.split(key, 16)
    f32 = jnp.float32

    def w(k, shape, fan_in):
        return jax.random.normal(k, shape, f32) * (fan_in ** -0.5)

    def gain(k, shape):
        return 1.0 + 0.02 * jax.random.normal(k, shape, f32)

    x = jax.random.normal(ks[0], (BATCH, SEQ, D_MODEL), f32)
    mem = jax.random.normal(ks[1], (BATCH, MEM_LEN, D_MODEL), f32)
    positions = jnp.broadcast_to(jnp.arange(SEQ, dtype=jnp.int32)[None, :], (BATCH, SEQ)).astype(jnp.int32)
    return {
        "x": x,
        "mem": mem,
        "positions": positions,
        "w_in_ret": w(ks[2], (N_RET, D_MODEL, RET_IN), D_MODEL),
        "w_in_fox": w(ks[3], (N_FOX, D_MODEL, FOX_IN), D_MODEL),
        "b_forget": 0.02 * jax.random.normal(ks[4], (N_FOX, FOX_HEADS), f32),
        "w_mem_kv": w(ks[5], (DEPTH, D_MODEL, 2 * MEM_WIDTH), D_MODEL),
        "w_out": w(ks[6], (DEPTH, D_MODEL, D_MODEL), D_MODEL),
        "w_up": w(ks[7], (DEPTH, D_MODEL, D_FF), D_MODEL),
        "w_down": w(ks[8], (DEPTH, D_FF, D_MODEL), D_FF),
        "g_pre_mix": gain(ks[9], (DEPTH, D_MODEL)),
        "g_post_mix": gain(ks[10], (DEPTH, D_MODEL)),
        "g_pre_mlp": gain(ks[11], (DEPTH, D_MODEL)),
        "g_post_mlp": gain(ks[12], (DEPTH, D_MODEL)),
        "g_mem": gain(ks[13], (DEPTH, D_MODEL)),
    }


def reference(x, mem, positions, w_in_ret, w_in_fox, b_forget, w_mem_kv, w_out, w_up, w_down,
              g_pre_mix, g_post_mix, g_pre_mlp, g_post_mlp, g_mem):
    B, S, _ = x.shape
    for i in range(DEPTH):
        mixer = i % N_MIXERS
        j = i // N_MIXERS
        h = rmsnorm(x, g_pre_mix[i])
        if mixer == 0:
            proj = h @ w_in_ret[j]
            q, k, v, gate, mq = jnp.split(proj, [RET_WIDTH, 2 * RET_WIDTH, 3 * RET_WIDTH, 4 * RET_WIDTH], axis=-1)
            q = rotary(q.reshape(B, S, RET_HEADS, RET_HEAD_DIM), positions)
            k = rotary(k.reshape(B, S, RET_HEADS, RET_HEAD_DIM), positions)
            v = v.reshape(B, S, RET_HEADS, RET_HEAD_DIM).astype(jnp.float32)
            r = retention_chunkwise(q, k, v)
            r = r * lax.rsqrt(jnp.mean(r * r, axis=-1, keepdims=True) + EPS)
            self_out = (r.reshape(B, S, RET_WIDTH) * jax.nn.silu(gate.astype(jnp.float32))).astype(x.dtype)
        else:
            proj = h @ w_in_fox[j]
            q, k, v, f_logit, mq = jnp.split(proj, [FOX_WIDTH, 2 * FOX_WIDTH, 3 * FOX_WIDTH, 3 * FOX_WIDTH + FOX_HEADS], axis=-1)
            log_f = jax.nn.log_sigmoid(f_logit.astype(jnp.float32) + b_forget[j].astype(jnp.float32))
            self_out = forgetting_attention(q.reshape(B, S, FOX_HEADS, FOX_HEAD_DIM),
                                            k.reshape(B, S, FOX_HEADS, FOX_HEAD_DIM),
                                            v.reshape(B, S, FOX_HEADS, FOX_HEAD_DIM), log_f)
            self_out = self_out.reshape(B, S, FOX_WIDTH).astype(x.dtype)
        mem_out = memory_attention(mq.reshape(B, S, MEM_HEADS, MEM_HEAD_DIM), mem, g_mem[i], w_mem_kv[i])
        mixed = jnp.concatenate([self_out, mem_out.reshape(B, S, MEM_WIDTH).astype(x.dtype)], axis=-1)
        x = x + rmsnorm(mixed @ w_out[i], g_post_mix[i])
        h = rmsnorm(x, g_pre_mlp[i])
        u = jax.nn.relu(h @ w_up[i])
        x = x + rmsnorm((u * u) @ w_down[i], g_post_mlp[i])
    return x
```

```python
import math
from contextlib import ExitStack

import numpy as np
import concourse.bass as bass
import concourse.mybir as mybir
from concourse.bass_utils import run_bass_kernel_spmd

F32 = mybir.dt.float32
BF = mybir.dt.bfloat16
I32 = mybir.dt.int32
AF = mybir.ActivationFunctionType
ALU = mybir.AluOpType

D = 1024
SEQ = 2048
NCH = 8
NBLK = 4
BLK = 512
NT = 16
MEM = 256
EPS = 1e-6
NSLOT = 3
EARLY = True
STRICT = True
NEG = -240000.0

C_MASK = 0
C_KD = C_MASK + 768
C_U = C_KD + 6
C_ONES = C_U + 128
C_INVF = C_ONES + 128
C_EPS = C_INVF + 1
C_G = C_EPS + 1
C_BF = C_G + 80
C_PERM = C_BF + 12
NCST = C_PERM + 128
I_QD = 0
I_IDENT = 768
I_PERM = I_IDENT + 128
I_NEG = I_PERM + 128
I_ONES = I_NEG + 128
NINIT = I_ONES + 128
GAINS = ("g_pre_mix", "g_post_mix", "g_pre_mlp", "g_post_mlp", "g_mem")


class Res:
    __slots__ = ("name", "last_w", "readers", "dsem", "dcount", "excl", "tw", "trd")

    def __init__(self, name, excl=False):
        self.name = name
        self.excl = excl
        self.tw = None
        self.trd = []
        self.last_w = None
        self.readers = []
        self.dsem = None
        self.dcount = 0


class Op:
    __slots__ = ("eng", "fn", "deps", "signal", "tick", "is_dma", "dres", "dcount", "seq")


ENGS = ("pe", "act", "dve", "pool", "sp")


class Sched:
    def __init__(self, nc):
        self.nc = nc
        self.ops = []
        self.by_eng = {e: [] for e in ENGS}
        self.out_dmas = []

    def add(self, eng, fn, reads=(), writes=(), dma=False, ndma=1, dres=None, final=False):
        op = Op()
        op.eng = eng
        op.fn = fn
        op.signal = False
        op.tick = None
        op.is_dma = dma
        op.dres = None
        op.dcount = None
        op.seq = len(self.ops)
        raw_src = list(reads)
        true_w = list(writes)
        if any(r.excl for r in reads):
            writes = list(writes) + [r for r in reads if r.excl]
            reads = [r for r in reads if not r.excl]
        deps = set()
        for r in reads:
            if r.last_w is not None:
                deps.add(r.last_w)
        for r in writes:
            if r.last_w is not None:
                deps.add(r.last_w)
            for rd in r.readers:
                deps.add(rd)
        keep = []
        for d in deps:
            if d is op:
                continue
            if (not d.is_dma) and d.eng == eng and not dma:
                if eng == "pe":
                    continue
                if STRICT:
                    hz = any(r.tw is d for r in raw_src) or any(r.tw is d for r in true_w) or \
                        any(d in r.trd for r in true_w)
                    if not hz:
                        continue
                elif not any(r.last_w is d for r in raw_src):
                    continue
            keep.append(d)
        best = {}
        for d in keep:
            if d.is_dma:
                k_ = ("d", id(d.dres))
                if k_ not in best or best[k_].dcount < d.dcount:
                    best[k_] = d
            else:
                k_ = ("e", d.eng)
                if k_ not in best or best[k_].seq < d.seq:
                    best[k_] = d
        keep = list(best.values())
        op.deps = keep
        for d in keep:
            d.signal = True
        if dma:
            if dres is None:
                dres = writes[0] if writes else reads[0]
            op.dres = dres
            dres.dcount += ndma
            op.dcount = dres.dcount
        for r in reads:
            r.readers.append(op)
        for r in writes:
            r.last_w = op
            r.readers = []
        for r in raw_src:
            r.trd.append(op)
        for r in true_w:
            r.tw = op
            r.trd = []
        self.ops.append(op)
        self.by_eng[eng].append(op)
        if final:
            self.out_dmas.append(op)
        return op

    def emit(self, stack):
        nc = self.nc
        esem = {e: stack.enter_context(nc.semaphore("s_" + e)) for e in ENGS}
        for op in self.ops:
            if op.is_dma and op.dres.dsem is None:
                op.dres.dsem = stack.enter_context(nc.semaphore("d_" + op.dres.name))
        for e in ENGS:
            t = 0
            for op in self.by_eng[e]:
                if (not op.is_dma) and op.signal:
                    t += 1
                    op.tick = t
        block = stack.enter_context(nc.Block())

        def run_engine(e, engobj):
            waited = {}
            for op in self.by_eng[e]:
                need = {}
                for d in op.deps:
                    if d.is_dma:
                        key, val, sem = id(d.dres), 16 * d.dcount, d.dres.dsem
                    else:
                        key, val, sem = d.eng, d.tick, esem[d.eng]
                    if need.get(key, (0, None))[0] < val:
                        need[key] = (val, sem)
                for key, (val, sem) in need.items():
                    if waited.get(key, 0) >= val:
                        continue
                    engobj.wait_ge(sem, val)
                    waited[key] = val
                if op.is_dma:
                    op.fn(engobj, op.dres.dsem)
                else:
                    ins = op.fn(engobj)
                    if op.signal:
                        ins.then_inc(esem[e], 1)
            if e == "sp":
                for op in self.out_dmas:
                    engobj.wait_ge(op.dres.dsem, 16 * op.dres.dcount)

        def mk(e):
            return lambda engobj: run_engine(e, engobj)

        block.tensor(mk("pe"))
        block.scalar(mk("act"))
        block.vector(mk("dve"))
        block.gpsimd(mk("pool"))
        block.sync(mk("sp"))


class Ring:
    def __init__(self, items):
        self.items = items
        self.i = 0

    def next(self):
        it = self.items[self.i % len(self.items)]
        self.i += 1
        return it


def layer_units(l):
    u = [("mk", l, 0), ("mk", l, 1), ("mv", l, 0), ("mv", l, 1), ("mq", l, 0), ("mq", l, 1)]
    if l % 2 == 0:
        for h in range(6):
            u += [("q", l, h), ("k", l, h), ("v", l, h), ("g", l, h)]
    else:
        u += [("f", l, 0)]
        for p in range(6):
            u += [("q", l, p), ("k", l, p), ("v", l, p)]
    u += [("out", l, oc) for oc in range(8)]
    for half in range(2):
        u += [("up", l, f) for f in range(32)]
        u += [("down", l, oc * 4 + g) for oc in range(8) for g in range(4)]
    return u


def _unit(wm, c):
    blk = wm[:, c * 128:(c + 1) * 128]
    return np.ascontiguousarray(blk.reshape(8, 128, 128).transpose(1, 0, 2)).reshape(128, 1024)


def pack_wall(inp, layers):
    idx = {}
    units = []

    def put(key, arr):
        if key not in idx:
            idx[key] = len(units)
            units.append(arr)

    for l in layers:
        j = l // 2
        win = inp["w_in_ret"][j] if l % 2 == 0 else inp["w_in_fox"][j]
        mqo = 3072 if l % 2 == 0 else 2316
        for key in layer_units(l):
            if key in idx:
                continue
            name, _, i = key
            if name == "mk":
                a = _unit(inp["w_mem_kv"][l][:, 0:256], i)
            elif name == "mv":
                a = _unit(inp["w_mem_kv"][l][:, 256:512], i)
            elif name == "mq":
                a = _unit(win[:, mqo:mqo + 256], i)
            elif name == "q":
                a = _unit(win[:, 0:768], i)
            elif name == "k":
                a = _unit(win[:, 768:1536], i)
            elif name == "v":
                a = _unit(win[:, 1536:2304], i)
            elif name == "g":
                a = _unit(win[:, 2304:3072], i)
            elif name == "f":
                pad = np.zeros((1024, 128), np.float32)
                pad[:, 0:12] = win[:, 2304:2316]
                a = _unit(pad, 0)
            elif name == "out":
                a = _unit(inp["w_out"][l], i)
            elif name == "up":
                a = _unit(inp["w_up"][l], i)
            elif name == "down":
                oc, g = i // 4, i % 4
                a = _unit(inp["w_down"][l][g * 1024:(g + 1) * 1024, :], oc)
            put(key, a)
    wall = np.stack(units, axis=0).astype(np.float32)
    return wall, idx


def build_consts(inp):
    c = np.zeros((128, NCST), np.float32)
    ci = np.zeros((128, NINIT), np.float32)
    j = np.arange(128, dtype=np.float64)
    for h in range(6):
        lg = math.log(1.0 - 2.0 ** (-5.0 - h))
        m = (128.0 ** -0.5) * np.exp(-(j[:, None] + 1.0) * lg) * (j[:, None] <= j[None, :])
        c[:, C_MASK + h * 128:C_MASK + (h + 1) * 128] = m
        ci[:, I_QD + h * 128:I_QD + (h + 1) * 128] = np.exp((j[None, :] + 1.0) * lg)
        c[:, C_KD + h] = (128.0 ** -0.5) * np.exp((127.0 - j) * lg)
    c[:, C_U:C_U + 128] = (j[:, None] <= j[None, :])
    c[:, C_ONES:C_ONES + 128] = 1.0
    ci[:, I_IDENT:I_IDENT + 128] = np.eye(128)
    ci[:, I_ONES:I_ONES + 128] = 1.0
    perm = np.zeros((128, 128))
    for dp in range(64):
        perm[dp + 64, dp] = -1.0
        perm[dp, dp + 64] = 1.0
    ci[:, I_PERM:I_PERM + 128] = perm
    c[:, C_PERM:C_PERM + 128] = perm
    ci[:, I_NEG:I_NEG + 128] = np.where(j[:, None] > j[None, :], NEG, 0.0)
    invf = (10000.0 ** (-np.arange(0, 128, 2, dtype=np.float32) / np.float32(128))).astype(np.float32)
    c[:, C_INVF] = np.concatenate([invf, invf])
    c[:, C_EPS] = EPS
    for gi, g in enumerate(GAINS):
        for l in range(2):
            c[:, C_G + (gi * 2 + l) * 8:C_G + (gi * 2 + l + 1) * 8] = inp[g][l].reshape(8, 128).T
    c[:, C_BF:C_BF + 12] = np.broadcast_to(inp["b_forget"][0][None, :], (128, 12))
    return c, ci


def build_program(n_seq, layers, wall_idx, n_units, stop=99):
    nc = bass.Bass("TRN2", target_bir_lowering=False)
    xT_d = nc.dram_tensor("xT", [n_seq, NCH, 128, SEQ], F32, kind="ExternalInput").ap()
    memT_d = nc.dram_tensor("memT", [n_seq, NCH, 128, MEM], F32, kind="ExternalInput").ap()
    pos_d = nc.dram_tensor("pos", [n_seq, SEQ], I32, kind="ExternalInput").ap()
    wall_d = nc.dram_tensor("wall", [n_units, 128, 1024], F32, kind="ExternalInput").ap()
    cst_d = nc.dram_tensor("cst", [128, NCST], F32, kind="ExternalInput").ap()
    cin_d = nc.dram_tensor("cin", [128, NINIT], F32, kind="ExternalInput").ap()
    out_d = nc.dram_tensor("outT", [n_seq, NCH, 128, SEQ], F32, kind="ExternalOutput").ap()

    plan = []
    for s in range(n_seq):
        for l in layers:
            plan += [wall_idx[k] for k in layer_units(l)]

    with ExitStack() as st:
        S = Sched(nc)

        def sb(name, shape, dt):
            return st.enter_context(nc.sbuf_tensor(name, shape, dt))

        def ps(name, shape, dt):
            return st.enter_context(nc.psum_tensor(name, shape, dt))

        xT = sb("xT_s", [128, NCH, SEQ], F32)
        hT = sb("hT_s", [128, NCH, SEQ], BF)
        mixT = sb("mixT_s", [128, NCH, SEQ], BF)
        u2hi = sb("u2hi_s", [128, 16384], BF)
        tab = sb("tab_s", [128, 4096], BF)
        wr = sb("wr_s", [128, NSLOT, 1024], BF)
        cst = sb("cst_s", [128, NCST], F32)
        cbf = sb("cbf_s", [128, 4, 128], BF)
        qdb = sb("qdb_s", [128, 6, 128], BF)
        scrF = [sb(f"scrF{i}", [128, BLK], F32) for i in range(3)]
        scrB = [sb(f"scrB{i}", [128, BLK], BF) for i in range(4)]
        rsd = [sb(f"rsd{i}", [128, BLK], F32) for i in range(2)]
        small = sb("small_s", [128, 1280], F32)
        smallb = sb("smallb_s", [128, 512], BF)
        junk = sb("junk_s", [128, 2], F32)

        pbank = [ps(f"pb{i}", [128, BLK], F32) for i in range(8)]
        ptr = pbank[7][:].bitcast(BF)
        bankR = [Res(f"bank{i}", excl=True) for i in range(8)]
        smP = [(pbank[6][:, 0:128], bankR[6]), (pbank[7][:, 0:128], bankR[7])]
        trP = [(ptr[:, 0:128], bankR[7]), (pbank[6][:, 0:64].bitcast(BF), bankR[6])]
        smRing = Ring(smP)
        trRing = Ring(trP)
        Pring = Ring([(pbank[0], bankR[0]), (pbank[1], bankR[1])])
        Aring = Ring([(pbank[2], bankR[2]), (pbank[3], bankR[3])])
        Oring = Ring([(pbank[4], bankR[4]), (pbank[5], bankR[5])])
        P4ring = Ring([(pbank[i], bankR[i]) for i in range(4)])

        xR = [[Res(f"x{c}_{b}") for b in range(NBLK)] for c in range(NCH)]
        hR = [[Res(f"h{c}_{b}") for b in range(NBLK)] for c in range(NCH)]
        mR = [[Res(f"m{c}_{b}") for b in range(NBLK)] for c in range(NCH)]
        tabR = [Res(f"tab{i}") for i in range(4)]
        wrR = [Res(f"wr{i}") for i in range(NSLOT)]
        cstR = Res("cst")
        cbfR = Res("cbf")
        TM = Res("TM")
        scrFr = Ring([(scrF[i], Res(f"scrF{i}")) for i in range(3)])
        scrBr = Ring([(scrB[i], Res(f"scrB{i}")) for i in range(4)])
        rsdr = Ring([(rsd[i], Res(f"rsd{i}")) for i in range(2)])
        junkR = Res("junk")

        ones_bf = cbf[:, 0, :]
        ident_bf = cbf[:, 1, :]
        perm_bf = cbf[:, 2, :]
        neg_bf = cbf[:, 3, :]
        ones_f = cst[:, C_ONES:C_ONES + 128]
        U_f = cst[:, C_U:C_U + 128]
        eps_c = cst[:, C_EPS:C_EPS + 1]
        one_c = cst[:, C_ONES:C_ONES + 1]

        def gcol(gi, l, c):
            o = C_G + (gi * 2 + l) * 8 + c
            return cst[:, o:o + 1]

        class WRing:
            def __init__(self):
                self.issued = 0
                self.consumed = 0
                self.prefetch()

            def prefetch(self):
                while self.issued < len(plan) and self.issued < self.consumed + NSLOT:
                    i = self.issued
                    slot = i % NSLOT
                    src = wall_d[plan[i]]
                    dst = wr[:, slot, :]
                    S.add("pool", (lambda d_, s_: lambda e, sem: e.dma_start(out=d_, in_=s_).then_inc(sem, 16))(dst, src),
                          writes=[wrR[slot]], dma=True)
                    self.issued += 1

            def get(self, key):
                i = self.consumed
                assert plan[i] == wall_idx[key], (key, i)
                assert self.issued > i
                slot = i % NSLOT
                self.consumed += 1
                return wr[:, slot, :].rearrange("p (k m) -> p k m", k=8), wrR[slot]

            def done(self):
                self.prefetch()

        def mm(out, lhsT, rhs, start, stop, reads, writes):
            S.add("pe", lambda e: e.matmul(out, lhsT=lhsT, rhs=rhs, start=start, stop=stop), reads=reads, writes=writes)

        def act(out, in_, func, reads, writes, bias=None, scale=None):
            kw = {}
            if bias is not None:
                kw["bias"] = bias
            if scale is not None:
                kw["scale"] = scale
            S.add("act", lambda e: e.activation(out=out, in_=in_, func=func, **kw), reads=reads, writes=writes)

        def dve(fn, reads, writes):
            S.add("dve", fn, reads=reads, writes=writes)

        def tt(out, in0, in1, op, reads, writes, eng="dve"):
            S.add(eng, lambda e: e.tensor_tensor(out=out, in0=in0, in1=in1, op=op), reads=reads, writes=writes)

        def stt(out, in0, scalar, in1, op0, op1, reads, writes, eng="dve"):
            S.add(eng, lambda e: e.scalar_tensor_tensor(out=out, in0=in0, scalar=scalar, in1=in1, op0=op0, op1=op1),
                  reads=reads, writes=writes)

        def ts(out, in0, s1, s2, op0, op1, reads, writes, eng="dve"):
            if s2 is None:
                S.add(eng, lambda e: e.tensor_scalar(out=out, in0=in0, scalar1=s1, scalar2=None, op0=op0), reads=reads, writes=writes)
            else:
                S.add(eng, lambda e: e.tensor_scalar(out=out, in0=in0, scalar1=s1, scalar2=s2, op0=op0, op1=op1),
                      reads=reads, writes=writes)

        def cp(out, in_, reads, writes, eng="dve"):
            S.add(eng, lambda e: e.tensor_copy(out=out, in_=in_), reads=reads, writes=writes)

        def recip(out, in_, reads, writes):
            S.add("dve", lambda e: e.reciprocal(out=out, in_=in_), reads=reads, writes=writes)

        def bsl(b):
            return slice(b * BLK, (b + 1) * BLK)

        S.add("sp", lambda e, sem: e.dma_start(out=cst[:], in_=cst_d).then_inc(sem, 16), writes=[cstR], dma=True)
        S.add("sp", lambda e, sem: e.dma_start(out=xT[:, 0, 0:NINIT], in_=cin_d).then_inc(sem, 16), writes=xR[0], dma=True, dres=xR[0][0])
        for i, off in enumerate((I_ONES, I_IDENT, I_PERM, I_NEG)):
            cp(cbf[:, i, :], xT[:, 0, off:off + 128], xR[0], [cbfR])
        cp(qdb[:].rearrange("p h m -> p (h m)"), xT[:, 0, I_QD:I_QD + 768], xR[0], [cbfR])

        W = WRing()

        def rstd_from(ss_ap, ss_res, n, width=BLK):
            rs, rsR = rsdr.next()
            act(rs[:, 0:width], ss_ap, AF.Ln, [ss_res, cstR], [rsR], bias=eps_c, scale=1.0 / n)
            act(rs[:, 0:width], rs[:, 0:width], AF.Exp, [rsR], [rsR], scale=-0.5)
            return rs, rsR

        NBring = Ring([(pbank[6], bankR[6]), (pbank[7], bankR[7])])
        prenormed = set()
        sqRing = Ring([(scrF[i][:].bitcast(BF)[:, 0:BLK], scrFr.items[i][1]) for i in range(3)])

        def norm_groups(l, gi, b, db):
            st = {}

            def g_sq(c0):
                def f():
                    if c0 == 0:
                        if b >= 2:
                            flush()
                        st["ss"] = NBring.next()
                    ss, ssR = st["ss"]
                    sqs = []
                    for c in (c0, c0 + 1):
                        sq, sqR = sqRing.next()
                        act(sq, xT[:, c, bsl(b)], AF.Square, [xR[c][b]], [sqR])
                        sqs.append((c, sq, sqR))
                    for c, sq, sqR in sqs:
                        mm(ss[:], ones_bf, sq, c == 0, c == NCH - 1, [sqR, cbfR], [ssR])
                    if c0 + 2 == NCH:
                        st["rs"] = rstd_from(ss[:], ssR, D)
                return f

            def g_out(c0):
                def f():
                    rs, rsR = st["rs"]
                    for c in (c0, c0 + 1):
                        stt(hT[:, c, bsl(db)], xT[:, c, bsl(b)], gcol(gi, l, c), rs[:], ALU.mult, ALU.mult,
                            [xR[c][b], rsR, cstR], [hR[c][db]])
                return f

            return [g_sq(c0) for c0 in range(0, NCH, 2)] + [g_out(c0) for c0 in range(0, NCH, 2)]

        def norm_x(l, gi, blks, dst_blk0=0):
            for b in blks:
                if (l, gi, b) in prenormed:
                    prenormed.discard((l, gi, b))
                    continue
                for g in norm_groups(l, gi, b, b - dst_blk0):
                    g()
                drain(4)

        def proj_unit(key, src, srcR, blks, ring):
            wu, wR = W.get(key)
            outs = []
            for b in blks:
                pb, pR = ring.next()
                for k in range(NCH):
                    mm(pb[:], wu[:, k, :], src[:, k, bsl(b)], k == 0, k == NCH - 1, [wR, srcR[k][b]], [pR])
                outs.append((pb, pR, b))
                yield pb, pR, b
            W.done()

        qR = [Res(f"q{b}") for b in range(NBLK)]
        kR = [Res(f"k{b}") for b in range(NBLK)]
        memhR = [[qR[c // 2]] for c in range(NCH)]
        memT = u2hi[:, 12288:16384].bitcast(F32).rearrange("p (c m) -> p c m", c=NCH)
        mqT = u2hi[:, 8192:12288].rearrange("p (u t) -> p u t", u=2)
        memR = Res("memT")
        mqR = [[Res(f"mq{u}_{b}") for b in range(NBLK)] for u in range(2)]
        memhT = u2hi[:, 0:2048].rearrange("p (c m) -> p c m", c=NCH)
        mkT = sb("mkT_s", [128, 2, MEM], BF)
        mkR = [Res("mk0"), Res("mk1")]
        mvaug = sb("mvaug_s", [128, 2, 4, 128], BF)
        mvR = Res("mvaug")
        PTr = scrBr
        rl = sb("rl_s", [128, BLK], F32)
        rlR = Res("rl")

        S.add("dve", lambda e: e.memset(mvaug[:].rearrange("p a b c -> p (a b c)"), 1.0), writes=[mvR])

        def mem_attention(s, l):
            S.add("sp", lambda e, sem: [e.dma_start(out=memT[:, c, :], in_=memT_d[s, c]).then_inc(sem, 16) for c in range(NCH)],
                  reads=[TM], writes=[memR], dma=True, ndma=NCH)
            ss, ssR = Aring.next()
            for c in range(NCH):
                sq, sqR = scrBr.next()
                act(sq[:, 0:MEM], memT[:, c, :], AF.Square, [memR, TM], [sqR])
                mm(ss[:, 0:MEM], ones_bf, sq[:, 0:MEM], c == 0, c == NCH - 1, [sqR, cbfR], [ssR])
            rs, rsR = rstd_from(ss[:, 0:MEM], ssR, D, MEM)
            for c in range(NCH):
                stt(memhT[:, c, :], memT[:, c, :], gcol(4, l, c), rs[:, 0:MEM], ALU.mult, ALU.mult,
                    [memR, rsR, cstR, TM], [memhR[c][0]])
            if stop < 3.2:
                return
            for u in range(2):
                wu, wR = W.get(("mk", l, u))
                pb, pR = Pring.next()
                for k in range(NCH):
                    mm(pb[:, 0:MEM], wu[:, k, :], memhT[:, k, :], k == 0, k == NCH - 1, [wR, memhR[k][0]], [pR])
                act(mkT[:, u, :], pb[:, 0:MEM], AF.Copy, [pR], [mkR[u]])
                W.done()
            if stop < 3.3:
                return
            for u in range(2):
                wu, wR = W.get(("mv", l, u))
                for kt in range(2):
                    sp_, spR = smRing.next()
                    for k in range(NCH):
                        mm(sp_, memhT[:, k, kt * 128:(kt + 1) * 128], wu[:, k, :], k == 0, k == NCH - 1,
                           [wR, memhR[k][0]], [spR])
                    act(mvaug[:, kt, 2 * u, 0:64], sp_[:, 0:64], AF.Copy, [spR], [mvR])
                    cp(mvaug[:, kt, 2 * u + 1, 64:128], sp_[:, 64:128], [spR], [mvR])
                W.done()
            if stop < 3.4:
                return
            for u in range(2):
                for pb, pR, b in proj_unit(("mq", l, u), hT, hR, range(NBLK), Pring):
                    act(mqT[:, u, bsl(b)], pb[:], AF.Copy, [pR, TM], [mqR[u][b]])
            if stop < 3.5:
                return
            for u in range(2):
                for hh in range(2):
                    p0 = 64 * hh
                    for qb in range(NBLK):
                        ob, oR = Oring.next()
                        for kt in range(2):
                            sb_, sR = Aring.next()
                            mm(sb_[:], mkT[p0:p0 + 64, u, kt * 128:(kt + 1) * 128], mqT[p0:p0 + 64, u, bsl(qb)], True, True,
                               [mkR[u], mqR[u][qb], TM], [sR])
                            pt, ptR = PTr.next()
                            act(pt[:], sb_[:], AF.Exp, [sR], [ptR], scale=0.125)
                            mm(ob[:], mvaug[:, kt, 2 * u + hh, :], pt[:], kt == 0, kt == 1, [mvR, ptR], [oR])
                        finish_attn(ob, oR, hh, 6 + u, qb)

        def finish_attn(ob, oR, hh, chunk, qb):
            p0 = 64 * hh
            q0 = 64 - p0
            act(rl[p0:p0 + 64, :], ob[q0:q0 + 64, :], AF.Ln, [oR], [rlR])
            act(rl[p0:p0 + 64, :], rl[p0:p0 + 64, :], AF.Exp, [rlR], [rlR], scale=-1.0)
            tt(mixT[p0:p0 + 64, chunk, bsl(qb)], ob[p0:p0 + 64, :], rl[p0:p0 + 64, :], ALU.mult, [oR, rlR], [mR[chunk][qb]])

        cosT = u2hi[:, 8192:12288].bitcast(F32)
        sinT = u2hi[:, 12288:16384].bitcast(F32)
        tabC = [Res(f"tabC{b}") for b in range(NBLK)]
        tabS = [Res(f"tabS{b}") for b in range(NBLK)]
        perm_f = cst[:, C_PERM:C_PERM + 128]
        dbc = tab[:].bitcast(F32)
        posi = rl[:].bitcast(I32)
        posiR = rlR
        TWO_PI = 2.0 * math.pi
        CW1 = float(np.float32(6.28125))
        CW2 = float(np.float32(TWO_PI - 6.28125))

        def rotary_tables(s):
            for b in range(NBLK):
                S.add("sp", (lambda b_: lambda e, sem: e.dma_start(out=posi[:], in_=pos_d[s:s + 1, bsl(b_)].broadcast_to([128, BLK])).then_inc(sem, 16))(b),
                      writes=[posiR], dma=True)
                ang, angR = scrFr.next()
                cp(ang[:], posi[:], [posiR], [angR])
                ts(ang[:], ang[:], cst[:, C_INVF:C_INVF + 1], None, ALU.mult, None, [angR, cstR], [angR])
                kf, kfR = scrFr.next()
                ts(kf[:], ang[:], 1.0 / TWO_PI, None, ALU.mult, None, [angR], [kfR])
                cp(posi[:], kf[:], [kfR], [posiR])
                cp(kf[:], posi[:], [posiR], [kfR])
                stt(ang[:], kf[:], -CW1, ang[:], ALU.mult, ALU.add, [kfR, angR], [angR])
                stt(ang[:], kf[:], -CW2, ang[:], ALU.mult, ALU.add, [kfR, angR], [angR])
                for which, shift in ((0, 0.0), (1, math.pi / 2)):
                    y, yR = scrFr.next()
                    ts(y[:], ang[:], shift, None, ALU.add, None, [angR], [yR])
                    ts(kf[:], y[:], math.pi, -TWO_PI, ALU.is_gt, ALU.mult, [yR], [kfR])
                    tt(y[:], y[:], kf[:], ALU.add, [yR, kfR], [yR])
                    ts(kf[:], y[:], -math.pi, TWO_PI, ALU.is_lt, ALU.mult, [yR], [kfR])
                    tt(y[:], y[:], kf[:], ALU.add, [yR, kfR], [yR])
                    ts(y[:], y[:], 3.1415925, -3.1415925, ALU.min, ALU.max, [yR], [yR])
                    if which == 0:
                        act(sinT[:, bsl(b)], y[:], AF.Sin, [yR, TM], [tabS[b], memR])
                    else:
                        act(cosT[:, bsl(b)], y[:], AF.Sin, [yR, TM], [tabC[b], mqR[b // 2][2 * (b % 2)], mqR[b // 2][2 * (b % 2) + 1]])

        qTp = u2hi[:, 0:2048]
        kTr = u2hi[:, 2048:4096]
        ktok = u2hi[:, 4096:6144].rearrange("p (n c) -> p n c", c=128)
        vtok = u2hi[:, 6144:8192].rearrange("p (n c) -> p n c", c=128)
        ktokR = [Res(f"ktok{n}") for n in range(NT)]
        vtokR = [Res(f"vtok{n}") for n in range(NT)]
        AT = [smallb[:, i * 128:(i + 1) * 128] for i in range(2)]
        ATr = Ring([(AT[i], Res(f"AT{i}")) for i in range(2)])
        stbf = [smallb[:, 256 + i * 128:256 + (i + 1) * 128] for i in range(2)]
        stbfr = Ring([(stbf[i], Res(f"stbf{i}")) for i in range(2)])
        state = small[:, 0:128]
        stateR = Res("state")
        sgs = [(sb("sg0_s", [128, BLK], BF), Res("sg0")), (rl[:].bitcast(BF)[:, 0:BLK], rlR)]

        qf0 = small[:, 1024:1152]
        kf0 = small[:, 1152:1280]
        qf0R = Res("qf0")
        kf0R = Res("kf0")

        def rope_unit(key, l, dstT, dstR, decay_h):
            pend = None

            def tail(b, raw, rawR, t1, t1R):
                pp, ppR = Aring.next()
                mm(pp[:], perm_f, raw[:], True, True, [rawR, cstR], [ppR])
                t2, t2R = scrFr.next()
                tt(t2[:], pp[:], sinT[:, bsl(b)], ALU.mult, [ppR, tabS[b], TM], [t2R])
                if decay_h is None:
                    tt(dstT[:, bsl(b)], t1[:], t2[:], ALU.add, [t1R, t2R, TM], [dstR[b]])
                    if b == 0:
                        tt(kf0, t1[:, 0:128], t2[:, 0:128], ALU.add, [t1R, t2R], [kf0R], eng="pool")
                else:
                    tt(t1[:], t1[:], t2[:], ALU.add, [t1R, t2R], [t1R])
                    qd = qdb[:, decay_h:decay_h + 1, :].broadcast_to([128, 4, 128])
                    tt(dstT[:, bsl(b)].rearrange("p (n c) -> p n c", c=128), t1[:].rearrange("p (n c) -> p n c", c=128), qd,
                       ALU.mult, [t1R, cbfR, TM], [dstR[b]], eng="pool")
                    if b == 0:
                        tt(qf0, t1[:, 0:128], qdb[:, decay_h, :], ALU.mult, [t1R, cbfR], [qf0R], eng="pool")

            for pb, pR, b in proj_unit(key, hT, hR, range(NBLK), Pring):
                if pend is not None:
                    tail(*pend)
                raw, rawR = scrFr.next()
                act(raw[:], pb[:], AF.Copy, [pR], [rawR])
                t1, t1R = scrFr.next()
                tt(t1[:], pb[:], cosT[:, bsl(b)], ALU.mult, [pR, tabC[b], TM], [t1R])
                pend = (b, raw, rawR, t1, t1R)
            tail(*pend)

        def retention_layer(s, l):
            for h in range(6):
                gam = 1.0 - 2.0 ** (-5.0 - h)
                cd = gam ** 128
                rope_unit(("q", l, h), l, qTp, qR, h)
                rope_unit(("k", l, h), l, kTr, kR, None)
                for n in range(NT):
                    tp, tpR = trRing.next()
                    S.add("pe", (lambda tp_, n_: lambda e: e.transpose(out=tp_, in_=kTr[:, n_ * 128:(n_ + 1) * 128], identity=ident_bf))(tp, n),
                          reads=[kR[n // 4], cbfR, TM], writes=[tpR])
                    act(ktok[:, n, :], tp, AF.Copy, [tpR, cstR, TM], [ktokR[n]], scale=cst[:, C_KD + h:C_KD + h + 1])
                wu, wR = W.get(("v", l, h))
                for n in range(NT):
                    sp_, spR = smRing.next()
                    for k in range(NCH):
                        mm(sp_, hT[:, k, n * 128:(n + 1) * 128], wu[:, k, :], k == 0, k == NCH - 1, [wR, hR[k][n // 4]], [spR])
                    if n % 2 == 0:
                        act(vtok[:, n, :], sp_, AF.Copy, [spR, TM], [vtokR[n]])
                    else:
                        cp(vtok[:, n, :], sp_, [spR, TM], [vtokR[n]])
                W.done()
                gu, gR = W.get(("g", l, h))
                obs = {}
                sb_for = {}
                mask_h = cst[:, C_MASK + h * 128:C_MASK + (h + 1) * 128]

                def emit_o(n, at, atR):
                    b = n // 4
                    ob, oR = obs[b]
                    csl = slice((n % 4) * 128, (n % 4 + 1) * 128)
                    tsl = slice(n * 128, (n + 1) * 128)
                    mm(ob[:, csl], vtok[:, n, :], at, True, n == 0, [vtokR[n], atR, TM], [oR])
                    if n > 0:
                        sbf, sbfR = sb_for[n]
                        mm(ob[:, csl], sbf, qTp[:, tsl], False, True, [sbfR, qR[b], TM], [oR])
                    if n % 4 == 3:
                        rsq, rsqR = scrBr.next()
                        act(rsq[:], ob[:], AF.Square, [oR], [rsqR])
                        gnext = gate_pe(b + 1) if b + 1 < NBLK else None
                        ss, ssR = Aring.next()
                        mm(ss[:], ones_bf, rsq[:], True, True, [rsqR, cbfR], [ssR])
                        rs, rsR = rstd_from(ss[:], ssR, 128)
                        if gnext is not None:
                            gate_act(b + 1, *gnext)

                        def fin(b=b, ob=ob, oR=oR, rs=rs, rsR=rsR):
                            tmp, tmpR = scrFr.next()
                            tt(tmp[:], ob[:], rs[:], ALU.mult, [oR, rsR], [tmpR])
                            sgb, sgbR = sgs[b % 2]
                            tt(mixT[:, h, bsl(b)], tmp[:], sgb[:], ALU.mult, [tmpR, sgbR], [mR[h][b]], eng="pool")
                        late.append((n + 3, fin))

                def gate_pe(b):
                    gp, gpR = Pring.next()
                    for k in range(NCH):
                        mm(gp[:], gu[:, k, :], hT[:, k, bsl(b)], k == 0, k == NCH - 1, [gR, hR[k][b]], [gpR])
                    return gp, gpR

                def gate_act(b, gp, gpR):
                    e1, e1R = scrFr.next()
                    act(e1[:], gp[:], AF.Exp, [gpR], [e1R], scale=-1.0)
                    act(e1[:], e1[:], AF.Ln, [e1R, cstR], [e1R], bias=one_c, scale=1.0)
                    act(e1[:], e1[:], AF.Exp, [e1R], [e1R], scale=-1.0)
                    def fin(b=b, gp=gp, gpR=gpR, e1=e1, e1R=e1R):
                        sgb, sgbR = sgs[b % 2]
                        tt(sgb[:], gp[:], e1[:], ALU.mult, [gpR, e1R], [sgbR])
                    late.append((4 * b + 2, fin))

                late = []
                pend_o = None
                for n in range(NT):
                    b = n // 4
                    if n % 4 == 0:
                        obs[b] = Oring.next()
                        if b == 0:
                            gate_act(0, *gate_pe(0))
                    tsl = slice(n * 128, (n + 1) * 128)
                    sp_, spR = smRing.next()
                    if n == 0:
                        mm(sp_, kf0, qf0, True, True, [kf0R, qf0R], [spR])
                    else:
                        mm(sp_, kTr[:, tsl], qTp[:, tsl], True, True, [kR[b], qR[b], TM], [spR])
                    if n < NT - 1:
                        kv, kvR = smRing.next()
                        mm(kv, ktok[:, n, :], vtok[:, n, :], True, True, [ktokR[n], vtokR[n], TM], [kvR])
                    at, atR = ATr.next()
                    tt(at, sp_, mask_h, ALU.mult, [spR, cstR], [atR])
                    if pend_o is not None:
                        emit_o(*pend_o)
                    pend_o = (n, at, atR)
                    if n < NT - 1:
                        if n == 0:
                            cp(state, kv, [kvR], [stateR])
                        else:
                            stt(state, state, cd, kv, ALU.mult, ALU.add, [stateR, kvR], [stateR])
                        sbf, sbfR = stbfr.next()
                        cp(sbf, state, [stateR], [sbfR])
                        sb_for[n + 1] = (sbf, sbfR)
                    while late and late[0][0] <= n:
                        late.pop(0)[1]()
                emit_o(*pend_o)
                while late:
                    late.pop(0)[1]()
                W.done()

        vaug = u2hi[:, 4096:8192].rearrange("p (t h e) -> p t h e", t=NT, h=2)
        vaugR = [Res(f"vaug{n}") for n in range(NT)]
        nlogf = small[:, 128:128 + 192].rearrange("p (t h) -> p t h", h=12)
        carry = small[:, 320:320 + 192].rearrange("p (t h) -> p t h", h=12)
        dcol = small[:, 512:512 + 192].rearrange("p (t h) -> p t h", h=12)
        sprev = small[:, 704:704 + 192].rearrange("p (t h) -> p t h", h=12)
        zt = small[:, 960:972]
        ztR = Res("zt")
        nlogfR = Res("nlogf")
        carryR = Res("carry")
        dcolR = Res("dcol")
        sprevR = Res("sprev")
        drow = tab[0:12, 0:2048]
        drowR = Res("drow")
        dqsem = [Res("dqA"), Res("dqB")]
        qX = [u2hi[:, 0:2048], u2hi[:, 8192:10240]]
        kX = [u2hi[:, 2048:4096], u2hi[:, 10240:12288]]
        qXR = [qR, mqR[0]]
        kXR = [kR, mqR[1]]

        def fox_layer(s, l):
            S.add("dve", lambda e: e.memset(u2hi[:, 4096:8192], 1.0), reads=[TM], writes=vaugR)
            for hh in range(2):
                S.add("dve", (lambda hh_: lambda e: e.memset(kX[hh_][64:128, :], 0.0))(hh), reads=[TM], writes=kXR[hh])
                S.add("dve", (lambda hh_: lambda e: e.memset(kX[hh_][64:65, :], 1.0))(hh), reads=[TM], writes=kXR[hh])
                S.add("dve", (lambda hh_: lambda e: e.memset(qX[hh_][64:128, :], 0.0))(hh), reads=[TM], writes=qXR[hh])
            nl_flat = small[:, 128:320]
            ca_flat = small[:, 320:512]
            dc_flat = small[:, 512:704]
            fb, fbR = Aring.next()
            wu, wR = W.get(("f", l, 0))
            fl = fb[:, 0:192].rearrange("p (t h) -> p t h", h=12)
            for n in range(NT):
                for k in range(NCH):
                    mm(fl[:, n, :], hT[:, k, n * 128:(n + 1) * 128], wu[:, k, 0:12], k == 0, k == NCH - 1,
                       [wR, hR[k][n // 4]], [fbR])
            W.done()
            bfb = cst[:, C_BF:C_BF + 12].rearrange("p (o h) -> p o h", o=1).broadcast_to([128, NT, 12])
            tt(nlogf, fl, bfb, ALU.add, [fbR, cstR], [nlogfR])
            act(nl_flat, nl_flat, AF.Exp, [nlogfR], [nlogfR], scale=-1.0)
            act(nl_flat, nl_flat, AF.Ln, [nlogfR, cstR], [nlogfR], bias=one_c, scale=1.0)
            tb, tbR = Aring.next()
            mm(tb[:, 0:192], ones_f, nl_flat, True, True, [cstR, nlogfR], [tbR])
            db, dbR = Aring.next()
            mm(db[:, 0:192], U_f, nl_flat, True, True, [cstR, nlogfR], [dbR])
            S.add("dve", lambda e: e.memset(carry[:, 0, :], 0.0), writes=[carryR])
            S.add("dve", lambda e: e.memset(sprev[:, 0, :], 0.0), writes=[sprevR])
            for n in range(NT - 1):
                tt(carry[:, n + 1, :], carry[:, n, :], tb[:, n * 12:(n + 1) * 12], ALU.add, [tbR, carryR], [carryR])
                tt(sprev[:, n + 1, :], sprev[:, n, :], nlogf[:, n, :], ALU.add, [sprevR, nlogfR], [sprevR], eng="pool")
            tt(dc_flat, db[:, 0:192], ca_flat, ALU.add, [dbR, carryR], [dcolR])
            for g in range(NBLK):
                rb, rbR = P4ring.next()
                for t in range(4):
                    n = 4 * g + t
                    mm(rb[0:12, t * 128:(t + 1) * 128], nlogf[:, n, :], U_f, True, n == 0, [cstR, nlogfR], [rbR])
                    if n > 0:
                        mm(rb[0:12, t * 128:(t + 1) * 128], sprev[:, n, :], ones_f, False, True, [cstR, sprevR], [rbR])
                act(drow[:, bsl(g)], rb[0:12, :], AF.Copy, [rbR], [drowR], scale=-8.0)
            for p in range(6):
                for hh in range(2):
                    S.add("sp", (lambda hh_, h_: lambda e, sem: e.dma_start(out=qX[hh_][64:65, :], in_=tab[h_:h_ + 1, 0:2048]).then_inc(sem, 16))(hh, 2 * p + hh),
                          reads=[drowR, TM], writes=qXR[hh], dma=True, dres=dqsem[hh])
                for pb, pR, b in proj_unit(("q", l, p), hT, hR, range(NBLK), Pring):
                    cp(qX[0][0:64, bsl(b)], pb[0:64, :], [pR, TM], [qXR[0][b]])
                    cp(qX[1][0:64, bsl(b)], pb[64:128, :], [pR, TM], [qXR[1][b]])
                for pb, pR, b in proj_unit(("k", l, p), hT, hR, range(NBLK), Pring):
                    cp(kX[0][0:64, bsl(b)], pb[0:64, :], [pR, TM], [kXR[0][b]])
                    cp(kX[1][0:64, bsl(b)], pb[64:128, :], [pR, TM], [kXR[1][b]])
                wu, wR = W.get(("v", l, p))
                for n in range(NT):
                    sp_, spR = smRing.next()
                    for k in range(NCH):
                        mm(sp_, hT[:, k, n * 128:(n + 1) * 128], wu[:, k, :], k == 0, k == NCH - 1, [wR, hR[k][n // 4]], [spR])
                    cp(vaug[:, n, 0, 0:64], sp_[:, 0:64], [spR, TM], [vaugR[n]])
                    cp(vaug[:, n, 1, 64:128], sp_[:, 64:128], [spR, TM], [vaugR[n]])
                W.done()
                steps = [(hh, qb, kt) for hh in range(2) for qb in range(NBLK) for kt in range(4 * qb + 4)]
                obs = {}
                pend = []

                def emit_pv(hh, qb, kt, pt, ptR, c0, w):
                    ob, oR = obs[(hh, qb)]
                    nkt = 4 * qb + 4
                    mm(ob[:, c0:BLK], vaug[:, kt, hh, :], pt[:, 0:w], kt == 0, kt == nkt - 1, [vaugR[kt], ptR, TM], [oR])
                    if kt == nkt - 1:
                        finish_attn(ob, oR, hh, p, qb)

                for (hh, qb, kt) in steps:
                    h = 2 * p + hh
                    if kt == 0:
                        obs[(hh, qb)] = Oring.next()
                    r = max(0, kt - 4 * qb)
                    c0 = 128 * r
                    w = BLK - c0
                    sb_, sR = P4ring.next()
                    mm(sb_[:, 0:w], kX[hh][:, kt * 128:(kt + 1) * 128], qX[hh][:, qb * BLK + c0:(qb + 1) * BLK],
                       True, kt < 4 * qb, [kXR[hh][kt // 4], qXR[hh][qb], TM], [sR])
                    if kt >= 4 * qb:
                        mm(sb_[:, 0:128], ident_bf, neg_bf, False, True, [cbfR], [sR])
                    pt, ptR = PTr.next()
                    act(pt[:, 0:w], sb_[:, 0:w], AF.Exp, [sR, dcolR], [ptR], bias=dcol[:, kt, h:h + 1], scale=0.125)
                    pend.append((hh, qb, kt, pt, ptR, c0, w))
                    if len(pend) > 2:
                        emit_pv(*pend.pop(0))
                while pend:
                    emit_pv(*pend.pop(0))

        deferred = []

        def drain(k=1):
            for _ in range(k):
                if deferred:
                    deferred.pop(0)()

        def flush():
            while deferred:
                deferred.pop(0)()

        def rstd_inplace(bank, bR, n):
            act(bank[:], bank[:], AF.Ln, [bR, cstR], [bR], bias=eps_c, scale=1.0 / n)
            act(bank[:], bank[:], AF.Exp, [bR], [bR], scale=-0.5)

        def tail_pair(stage, stageR, gc, bank, bR, xap, xRes, oc):
            def f():
                t, tR = scrFr.next()
                stt(t[:], stage, gc, bank[:], ALU.mult, ALU.mult, [stageR, bR, cstR], [tR])
                tt(xap, xap, t[:], ALU.add, [xRes, tR], [xRes], eng=("pool" if oc % 2 else "dve"))
            return f

        def out_proj(l):
            ssb = [(pbank[2 + b], bankR[2 + b]) for b in range(NBLK)]
            pend = None
            for oc in range(NCH):
                for pb, pR, b in proj_unit(("out", l, oc), mixT, mR, range(NBLK), Pring):
                    cp(hT[:, oc, bsl(b)], pb[:], [pR], [hR[oc][b]])
                    sq, sqR = scrBr.next()
                    act(sq[:], pb[:], AF.Square, [pR], [sqR])
                    if pend is not None:
                        mm(*pend)
                    pend = (ssb[b][0][:], ones_bf, sq[:], oc == 0, oc == NCH - 1, [sqR, cbfR], [ssb[b][1]])
            mm(*pend)
            for b in range(NBLK):
                rstd_inplace(ssb[b][0], ssb[b][1], D)
            for b in range(NBLK):
                for oc in range(NCH):
                    f = tail_pair(hT[:, oc, bsl(b)], hR[oc][b], gcol(1, l, oc), ssb[b][0], ssb[b][1], xT[:, oc, bsl(b)], xR[oc][b], oc)
                    if b < 2:
                        f()
                    else:
                        deferred.append(f)

        def u2T(f, hb):
            if f < 16:
                return mixT[:].rearrange("p c t -> p (c t)")[:, f * 1024 + hb * BLK:f * 1024 + (hb + 1) * BLK]
            return u2hi[:, (f - 16) * 1024 + hb * BLK:(f - 16) * 1024 + (hb + 1) * BLK]

        u2R = [[Res(f"u2_{f}_{hb}") for hb in range(2)] for f in range(32)]
        for f in range(16):
            for hb in range(2):
                u2R[f][hb] = mR[f // 2][(f % 2) * 2 + hb]

        def mlp(l, after_half0=None, next_layer=None):
            S.add("dve", lambda e: e.memset(junk[:], 0.0), writes=[TM, junkR])
            for half in range(2):
                blks = [2 * half, 2 * half + 1]
                norm_x(l, 2, blks, dst_blk0=2 * half)
                hRl = [[hR[c][0], hR[c][1]] for c in range(NCH)]
                for f in range(32):
                    tmr = [TM] if f >= 16 else []
                    for pb, pR, b in proj_unit(("up", l, f), hT, hRl, range(2), P4ring):
                        r, rR = scrBr.next()
                        act(r[:], pb[:], AF.Relu, [pR], [rR])
                        tt(u2T(f, b), r[:], r[:], ALU.mult, [rR] + tmr, [u2R[f][b]])
                    drain(1)
                flush()
                if half == 1 and after_half0 is not None:
                    after_half0()
                ssb = [(pbank[4 + b], bankR[4 + b]) for b in range(2)]
                pend = None
                early = []
                if half == 0 and EARLY:
                    for b in (2, 3):
                        early += norm_groups(l, 2, b, b - 2)
                        prenormed.add((l, 2, b))
                elif next_layer is not None and EARLY:
                    for b in (0, 1):
                        early += norm_groups(next_layer, 0, b, b)
                        prenormed.add((next_layer, 0, b))
                for oc in range(NCH):
                    acc = [P4ring.next() for _ in range(2)]
                    for g in range(4):
                        if oc >= 1 and early:
                            early.pop(0)()
                        wu, wR = W.get(("down", l, oc * 4 + g))
                        for k in range(NCH):
                            f = g * 8 + k
                            tmr = [TM] if f >= 16 else []
                            for b in range(2):
                                mm(acc[b][0][:], wu[:, k, :], u2T(f, b), f == 0, f == 31, [wR, u2R[f][b]] + tmr, [acc[b][1]])
                        W.done()
                    for b in range(2):
                        cp(hT[:, oc, bsl(2 + b)], acc[b][0][:], [acc[b][1]], [hR[oc][2 + b]])
                        sq, sqR = scrBr.next()
                        act(sq[:], acc[b][0][:], AF.Square, [acc[b][1]], [sqR])
                        if pend is not None:
                            mm(*pend)
                        pend = (ssb[b][0][:], ones_bf, sq[:], oc == 0, oc == NCH - 1, [sqR, cbfR], [ssb[b][1]])
                mm(*pend)
                while early:
                    early.pop(0)()
                for b in range(2):
                    rstd_inplace(ssb[b][0], ssb[b][1], D)
                for b in range(2):
                    xb = 2 * half + b
                    for oc in range(NCH):
                        deferred.append(tail_pair(hT[:, oc, bsl(2 + b)], hR[oc][2 + b], gcol(3, l, oc), ssb[b][0], ssb[b][1],
                                                  xT[:, oc, bsl(xb)], xR[oc][xb], oc))
            S.add("dve", lambda e: e.memset(junk[:], 0.0), writes=[TM, junkR])

        xsem = [Res(f"xsem{b}") for b in range(NBLK)]

        def load_x(s, b):
            S.add("sp", lambda e, sem: e.dma_start(out=xT[:, :, bsl(b)], in_=xT_d[s][:, :, bsl(b)].rearrange("c p t -> p c t")).then_inc(sem, 16),
                  writes=[xR[c][b] for c in range(NCH)], dma=True, dres=xsem[b])

        def store_x(s, b):
            S.add("sp", lambda e, sem: e.dma_start(out=out_d[s][:, :, bsl(b)].rearrange("c p t -> p c t"), in_=xT[:, :, bsl(b)]).then_inc(sem, 16),
                  reads=[xR[c][b] for c in range(NCH)], dma=True, dres=xsem[b], final=True)

        xstate = {"early": None}

        def early_io(s):
            for b in range(2):
                store_x(s, b)
            if s + 1 < n_seq:
                for b in range(2):
                    load_x(s + 1, b)
            xstate["early"] = s
        for s in range(n_seq):
            for b in range(NBLK):
                if not (s > 0 and b < 2):
                    load_x(s, b)
            for l in layers:
                if stop >= 2:
                    norm_x(l, 0, range(NBLK))
                if stop >= 3:
                    mem_attention(s, l)
                if l % 2 == 0 and stop >= 1:
                    rotary_tables(s)
                if stop >= 4:
                    if l % 2 == 0:
                        retention_layer(s, l)
                    else:
                        fox_layer(s, l)
                if stop >= 5:
                    out_proj(l)
                if stop >= 6:
                    last = (l == layers[-1])
                    mlp(l, (lambda s_=s: early_io(s_)) if last else None, None if last else layers[layers.index(l) + 1])
            flush()
            for b in range(NBLK):
                if not (xstate["early"] == s and b < 2):
                    store_x(s, b)
        assert stop < 6 or W.consumed == len(plan), (W.consumed, len(plan))
        S.emit(st)
    return nc


_CACHE = {}


def kernel(**inp):
    inp = {k: np.asarray(v) for k, v in inp.items()}
    n_cores = 8
    n_seq = 2
    layers = (0, 1)
    wall, widx = pack_wall(inp, layers)
    cst, cin = build_consts(inp)
    key = ("prog", n_seq, layers, wall.shape[0])
    if key not in _CACHE:
        _CACHE[key] = build_program(n_seq, layers, widx, wall.shape[0])
    nc = _CACHE[key]
    x = inp["x"].astype(np.float32, copy=False)
    mem = inp["mem"].astype(np.float32, copy=False)
    pos = inp["positions"].astype(np.int32, copy=False)
    in_maps = []
    for c in range(n_cores):
        sl = slice(c * n_seq, (c + 1) * n_seq)
        xT = np.ascontiguousarray(x[sl].transpose(0, 2, 1)).reshape(n_seq, NCH, 128, SEQ)
        mT = np.ascontiguousarray(mem[sl].transpose(0, 2, 1)).reshape(n_seq, NCH, 128, MEM)
        in_maps.append({"xT": xT, "memT": mT, "pos": np.ascontiguousarray(pos[sl]), "wall": wall, "cst": cst, "cin": cin})
    res = run_bass_kernel_spmd(nc, in_maps, core_ids=list(range(n_cores)))
    outs = []
    for c in range(n_cores):
        o = np.asarray(res.results[c]["outT"]).reshape(n_seq, D, SEQ).transpose(0, 2, 1)
        outs.append(o)
    return np.ascontiguousarray(np.concatenate(outs, axis=0)).astype(np.float32, copy=False)
```

```python
import math
from contextlib import ExitStack

import numpy as np
import concourse.bass as bass
import concourse.mybir as mybir
from concourse.bass_utils import run_bass_kernel_spmd

F32 = mybir.dt.float32
BF = mybir.dt.bfloat16
I32 = mybir.dt.int32
AF = mybir.ActivationFunctionType
ALU = mybir.AluOpType

D = 1024
SEQ = 2048
NCH = 8
NBLK = 4
BLK = 512
NT = 16
MEM = 256
EPS = 1e-6
NSLOT = 3
EARLY = True
STRICT = True
NEG = -240000.0

C_MASK = 0
C_KD = C_MASK + 768
C_U = C_KD + 6
C_ONES = C_U + 128
C_INVF = C_ONES + 128
C_EPS = C_INVF + 1
C_G = C_EPS + 1
C_BF = C_G + 80
C_PERM = C_BF + 12
NCST = C_PERM + 128
I_QD = 0
I_IDENT = 768
I_PERM = I_IDENT + 128
I_NEG = I_PERM + 128
I_ONES = I_NEG + 128
NINIT = I_ONES + 128
GAINS = ("g_pre_mix", "g_post_mix", "g_pre_mlp", "g_post_mlp", "g_mem")


class Res:
    __slots__ = ("name", "last_w", "readers", "dsem", "dcount", "excl", "tw", "trd")

    def __init__(self, name, excl=False):
        self.name = name
        self.excl = excl
        self.tw = None
        self.trd = []
        self.last_w = None
        self.readers = []
        self.dsem = None
        self.dcount = 0


class Op:
    __slots__ = ("eng", "fn", "deps", "signal", "tick", "is_dma", "dres", "dcount", "seq")


ENGS = ("pe", "act", "dve", "pool", "sp")


class Sched:
    def __init__(self, nc):
        self.nc = nc
        self.ops = []
        self.by_eng = {e: [] for e in ENGS}
        self.out_dmas = []

    def add(self, eng, fn, reads=(), writes=(), dma=False, ndma=1, dres=None, final=False):
        op = Op()
        op.eng = eng
        op.fn = fn
        op.signal = False
        op.tick = None
        op.is_dma = dma
        op.dres = None
        op.dcount = None
        op.seq = len(self.ops)
        raw_src = list(reads)
        true_w = list(writes)
        if any(r.excl for r in reads):
            writes = list(writes) + [r for r in reads if r.excl]
            reads = [r for r in reads if not r.excl]
        deps = set()
        for r in reads:
            if r.last_w is not None:
                deps.add(r.last_w)
        for r in writes:
            if r.last_w is not None:
                deps.add(r.last_w)
            for rd in r.readers:
                deps.add(rd)
        keep = []
        for d in deps:
            if d is op:
                continue
            if (not d.is_dma) and d.eng == eng and not dma:
                if eng == "pe":
                    continue
                if STRICT:
                    hz = any(r.tw is d for r in raw_src) or any(r.tw is d for r in true_w) or \
                        any(d in r.trd for r in true_w)
                    if not hz:
                        continue
                elif not any(r.last_w is d for r in raw_src):
                    continue
            keep.append(d)
        best = {}
        for d in keep:
            if d.is_dma:
                k_ = ("d", id(d.dres))
                if k_ not in best or best[k_].dcount < d.dcount:
                    best[k_] = d
            else:
                k_ = ("e", d.eng)
                if k_ not in best or best[k_].seq < d.seq:
                    best[k_] = d
        keep = list(best.values())
        op.deps = keep
        for d in keep:
            d.signal = True
        if dma:
            if dres is None:
                dres = writes[0] if writes else reads[0]
            op.dres = dres
            dres.dcount += ndma
            op.dcount = dres.dcount
        for r in reads:
            r.readers.append(op)
        for r in writes:
            r.last_w = op
            r.readers = []
        for r in raw_src:
            r.trd.append(op)
        for r in true_w:
            r.tw = op
            r.trd = []
        self.ops.append(op)
        self.by_eng[eng].append(op)
        if final:
            self.out_dmas.append(op)
        return op

    def emit(self, stack):
        nc = self.nc
        esem = {e: stack.enter_context(nc.semaphore("s_" + e)) for e in ENGS}
        for op in self.ops:
            if op.is_dma and op.dres.dsem is None:
                op.dres.dsem = stack.enter_context(nc.semaphore("d_" + op.dres.name))
        for e in ENGS:
            t = 0
            for op in self.by_eng[e]:
                if (not op.is_dma) and op.signal:
                    t += 1
                    op.tick = t
        block = stack.enter_context(nc.Block())

        def run_engine(e, engobj):
            waited = {}
            for op in self.by_eng[e]:
                need = {}
                for d in op.deps:
                    if d.is_dma:
                        key, val, sem = id(d.dres), 16 * d.dcount, d.dres.dsem
                    else:
                        key, val, sem = d.eng, d.tick, esem[d.eng]
                    if need.get(key, (0, None))[0] < val:
                        need[key] = (val, sem)
                for key, (val, sem) in need.items():
                    if waited.get(key, 0) >= val:
                        continue
                    engobj.wait_ge(sem, val)
                    waited[key] = val
                if op.is_dma:
                    op.fn(engobj, op.dres.dsem)
                else:
                    ins = op.fn(engobj)
                    if op.signal:
                        ins.then_inc(esem[e], 1)
            if e == "sp":
                for op in self.out_dmas:
                    engobj.wait_ge(op.dres.dsem, 16 * op.dres.dcount)

        def mk(e):
            return lambda engobj: run_engine(e, engobj)

        block.tensor(mk("pe"))
        block.scalar(mk("act"))
        block.vector(mk("dve"))
        block.gpsimd(mk("pool"))
        block.sync(mk("sp"))


class Ring:
    def __init__(self, items):
        self.items = items
        self.i = 0

    def next(self):
        it = self.items[self.i % len(self.items)]
        self.i += 1
        return it


def layer_units(l):
    u = [("mk", l, 0), ("mk", l, 1), ("mv", l, 0), ("mv", l, 1), ("mq", l, 0), ("mq", l, 1)]
    if l % 2 == 0:
        for h in range(6):
            u += [("q", l, h), ("k", l, h), ("v", l, h), ("g", l, h)]
    else:
        u += [("f", l, 0)]
        for p in range(6):
            u += [("q", l, p), ("k", l, p), ("v", l, p)]
    u += [("out", l, oc) for oc in range(8)]
    for half in range(2):
        u += [("up", l, f) for f in range(32)]
        u += [("down", l, oc * 4 + g) for oc in range(8) for g in range(4)]
    return u


def _unit(wm, c):
    blk = wm[:, c * 128:(c + 1) * 128]
    return np.ascontiguousarray(blk.reshape(8, 128, 128).transpose(1, 0, 2)).reshape(128, 1024)


def pack_wall(inp, layers):
    idx = {}
    units = []

    def put(key, arr):
        if key not in idx:
            idx[key] = len(units)
            units.append(arr)

    for l in layers:
        j = l // 2
        win = inp["w_in_ret"][j] if l % 2 == 0 else inp["w_in_fox"][j]
        mqo = 3072 if l % 2 == 0 else 2316
        for key in layer_units(l):
            if key in idx:
                continue
            name, _, i = key
            if name == "mk":
                a = _unit(inp["w_mem_kv"][l][:, 0:256], i)
            elif name == "mv":
                a = _unit(inp["w_mem_kv"][l][:, 256:512], i)
            elif name == "mq":
                a = _unit(win[:, mqo:mqo + 256], i)
            elif name == "q":
                a = _unit(win[:, 0:768], i)
            elif name == "k":
                a = _unit(win[:, 768:1536], i)
            elif name == "v":
                a = _unit(win[:, 1536:2304], i)
            elif name == "g":
                a = _unit(win[:, 2304:3072], i)
            elif name == "f":
                pad = np.zeros((1024, 128), np.float32)
                pad[:, 0:12] = win[:, 2304:2316]
                a = _unit(pad, 0)
            elif name == "out":
                a = _unit(inp["w_out"][l], i)
            elif name == "up":
                a = _unit(inp["w_up"][l], i)
            elif name == "down":
                oc, g = i // 4, i % 4
                a = _unit(inp["w_down"][l][g * 1024:(g + 1) * 1024, :], oc)
            put(key, a)
    wall = np.stack(units, axis=0).astype(np.float32)
    return wall, idx


def build_consts(inp):
    c = np.zeros((128, NCST), np.float32)
    ci = np.zeros((128, NINIT), np.float32)
    j = np.arange(128, dtype=np.float64)
    for h in range(6):
        lg = math.log(1.0 - 2.0 ** (-5.0 - h))
        m = (128.0 ** -0.5) * np.exp(-(j[:, None] + 1.0) * lg) * (j[:, None] <= j[None, :])
        c[:, C_MASK + h * 128:C_MASK + (h + 1) * 128] = m
        ci[:, I_QD + h * 128:I_QD + (h + 1) * 128] = np.exp((j[None, :] + 1.0) * lg)
        c[:, C_KD + h] = (128.0 ** -0.5) * np.exp((127.0 - j) * lg)
    c[:, C_U:C_U + 128] = (j[:, None] <= j[None, :])
    c[:, C_ONES:C_ONES + 128] = 1.0
    ci[:, I_IDENT:I_IDENT + 128] = np.eye(128)
    ci[:, I_ONES:I_ONES + 128] = 1.0
    perm = np.zeros((128, 128))
    for dp in range(64):
        perm[dp + 64, dp] = -1.0
        perm[dp, dp + 64] = 1.0
    ci[:, I_PERM:I_PERM + 128] = perm
    c[:, C_PERM:C_PERM + 128] = perm
    ci[:, I_NEG:I_NEG + 128] = np.where(j[:, None] > j[None, :], NEG, 0.0)
    invf = (10000.0 ** (-np.arange(0, 128, 2, dtype=np.float32) / np.float32(128))).astype(np.float32)
    c[:, C_INVF] = np.concatenate([invf, invf])
    c[:, C_EPS] = EPS
    for gi, g in enumerate(GAINS):
        for l in range(2):
            c[:, C_G + (gi * 2 + l) * 8:C_G + (gi * 2 + l + 1) * 8] = inp[g][l].reshape(8, 128).T
    c[:, C_BF:C_BF + 12] = np.broadcast_to(inp["b_forget"][0][None, :], (128, 12))
    return c, ci


def build_program(n_seq, layers, wall_idx, n_units, stop=99):
    nc = bass.Bass("TRN2", target_bir_lowering=False)
    xT_d = nc.dram_tensor("xT", [n_seq, NCH, 128, SEQ], F32, kind="ExternalInput").ap()
    memT_d = nc.dram_tensor("memT", [n_seq, NCH, 128, MEM], F32, kind="ExternalInput").ap()
    pos_d = nc.dram_tensor("pos", [n_seq, SEQ], I32, kind="ExternalInput").ap()
    wall_d = nc.dram_tensor("wall", [n_units, 128, 1024], F32, kind="ExternalInput").ap()
    cst_d = nc.dram_tensor("cst", [128, NCST], F32, kind="ExternalInput").ap()
    cin_d = nc.dram_tensor("cin", [128, NINIT], F32, kind="ExternalInput").ap()
    out_d = nc.dram_tensor("outT", [n_seq, NCH, 128, SEQ], F32, kind="ExternalOutput").ap()

    plan = []
    for s in range(n_seq):
        for l in layers:
            plan += [wall_idx[k] for k in layer_units(l)]

    with ExitStack() as st:
        S = Sched(nc)

        def sb(name, shape, dt):
            return st.enter_context(nc.sbuf_tensor(name, shape, dt))

        def ps(name, shape, dt):
            return st.enter_context(nc.psum_tensor(name, shape, dt))

        xT = sb("xT_s", [128, NCH, SEQ], F32)
        hT = sb("hT_s", [128, NCH, SEQ], BF)
        mixT = sb("mixT_s", [128, NCH, SEQ], BF)
        u2hi = sb("u2hi_s", [128, 16384], BF)
        tab = sb("tab_s", [128, 4096], BF)
        wr = sb("wr_s", [128, NSLOT, 1024], BF)
        cst = sb("cst_s", [128, NCST], F32)
        cbf = sb("cbf_s", [128, 4, 128], BF)
        qdb = sb("qdb_s", [128, 6, 128], BF)
        scrF = [sb(f"scrF{i}", [128, BLK], F32) for i in range(3)]
        scrB = [sb(f"scrB{i}", [128, BLK], BF) for i in range(4)]
        rsd = [sb(f"rsd{i}", [128, BLK], F32) for i in range(2)]
        small = sb("small_s", [128, 1280], F32)
        smallb = sb("smallb_s", [128, 512], BF)
        junk = sb("junk_s", [128, 2], F32)

        pbank = [ps(f"pb{i}", [128, BLK], F32) for i in range(8)]
        ptr = pbank[7][:].bitcast(BF)
        bankR = [Res(f"bank{i}", excl=True) for i in range(8)]
        smP = [(pbank[6][:, 0:128], bankR[6]), (pbank[7][:, 0:128], bankR[7])]
        trP = [(ptr[:, 0:128], bankR[7]), (pbank[6][:, 0:64].bitcast(BF), bankR[6])]
        smRing = Ring(smP)
        trRing = Ring(trP)
        Pring = Ring([(pbank[0], bankR[0]), (pbank[1], bankR[1])])
        Aring = Ring([(pbank[2], bankR[2]), (pbank[3], bankR[3])])
        Oring = Ring([(pbank[4], bankR[4]), (pbank[5], bankR[5])])
        P4ring = Ring([(pbank[i], bankR[i]) for i in range(4)])

        xR = [[Res(f"x{c}_{b}") for b in range(NBLK)] for c in range(NCH)]
        hR = [[Res(f"h{c}_{b}") for b in range(NBLK)] for c in range(NCH)]
        mR = [[Res(f"m{c}_{b}") for b in range(NBLK)] for c in range(NCH)]
        tabR = [Res(f"tab{i}") for i in range(4)]
        wrR = [Res(f"wr{i}") for i in range(NSLOT)]
        cstR = Res("cst")
        cbfR = Res("cbf")
        TM = Res("TM")
        scrFr = Ring([(scrF[i], Res(f"scrF{i}")) for i in range(3)])
        scrBr = Ring([(scrB[i], Res(f"scrB{i}")) for i in range(4)])
        rsdr = Ring([(rsd[i], Res(f"rsd{i}")) for i in range(2)])
        junkR = Res("junk")

        ones_bf = cbf[:, 0, :]
        ident_bf = cbf[:, 1, :]
        perm_bf = cbf[:, 2, :]
        neg_bf = cbf[:, 3, :]
        ones_f = cst[:, C_ONES:C_ONES + 128]
        U_f = cst[:, C_U:C_U + 128]
        eps_c = cst[:, C_EPS:C_EPS + 1]
        one_c = cst[:, C_ONES:C_ONES + 1]

        def gcol(gi, l, c):
            o = C_G + (gi * 2 + l) * 8 + c
            return cst[:, o:o + 1]

        class WRing:
            def __init__(self):
                self.issued = 0
                self.consumed = 0
                self.prefetch()

            def prefetch(self):
                while self.issued < len(plan) and self.issued < self.consumed + NSLOT:
                    i = self.issued
                    slot = i % NSLOT
                    src = wall_d[plan[i]]
                    dst = wr[:, slot, :]
                    S.add("pool", (lambda d_, s_: lambda e, sem: e.dma_start(out=d_, in_=s_).then_inc(sem, 16))(dst, src),
                          writes=[wrR[slot]], dma=True)
                    self.issued += 1

            def get(self, key):
                i = self.consumed
                assert plan[i] == wall_idx[key], (key, i)
                assert self.issued > i
                slot = i % NSLOT
                self.consumed += 1
                return wr[:, slot, :].rearrange("p (k m) -> p k m", k=8), wrR[slot]

            def done(self):
                self.prefetch()

        def mm(out, lhsT, rhs, start, stop, reads, writes):
            S.add("pe", lambda e: e.matmul(out, lhsT=lhsT, rhs=rhs, start=start, stop=stop), reads=reads, writes=writes)

        def act(out, in_, func, reads, writes, bias=None, scale=None):
            kw = {}
            if bias is not None:
                kw["bias"] = bias
            if scale is not None:
                kw["scale"] = scale
            S.add("act", lambda e: e.activation(out=out, in_=in_, func=func, **kw), reads=reads, writes=writes)

        def dve(fn, reads, writes):
            S.add("dve", fn, reads=reads, writes=writes)

        def tt(out, in0, in1, op, reads, writes, eng="dve"):
            S.add(eng, lambda e: e.tensor_tensor(out=out, in0=in0, in1=in1, op=op), reads=reads, writes=writes)

        def stt(out, in0, scalar, in1, op0, op1, reads, writes, eng="dve"):
            S.add(eng, lambda e: e.scalar_tensor_tensor(out=out, in0=in0, scalar=scalar, in1=in1, op0=op0, op1=op1),
                  reads=reads, writes=writes)

        def ts(out, in0, s1, s2, op0, op1, reads, writes, eng="dve"):
            if s2 is None:
                S.add(eng, lambda e: e.tensor_scalar(out=out, in0=in0, scalar1=s1, scalar2=None, op0=op0), reads=reads, writes=writes)
            else:
                S.add(eng, lambda e: e.tensor_scalar(out=out, in0=in0, scalar1=s1, scalar2=s2, op0=op0, op1=op1),
                      reads=reads, writes=writes)

        def cp(out, in_, reads, writes, eng="dve"):
            S.add(eng, lambda e: e.tensor_copy(out=out, in_=in_), reads=reads, writes=writes)

        def recip(out, in_, reads, writes):
            S.add("dve", lambda e: e.reciprocal(out=out, in_=in_), reads=reads, writes=writes)

        def bsl(b):
            return slice(b * BLK, (b + 1) * BLK)

        S.add("sp", lambda e, sem: e.dma_start(out=cst[:], in_=cst_d).then_inc(sem, 16), writes=[cstR], dma=True)
        S.add("sp", lambda e, sem: e.dma_start(out=xT[:, 0, 0:NINIT], in_=cin_d).then_inc(sem, 16), writes=xR[0], dma=True, dres=xR[0][0])
        for i, off in enumerate((I_ONES, I_IDENT, I_PERM, I_NEG)):
            cp(cbf[:, i, :], xT[:, 0, off:off + 128], xR[0], [cbfR])
        cp(qdb[:].rearrange("p h m -> p (h m)"), xT[:, 0, I_QD:I_QD + 768], xR[0], [cbfR])

        W = WRing()

        def rstd_from(ss_ap, ss_res, n, width=BLK):
            rs, rsR = rsdr.next()
            act(rs[:, 0:width], ss_ap, AF.Ln, [ss_res, cstR], [rsR], bias=eps_c, scale=1.0 / n)
            act(rs[:, 0:width], rs[:, 0:width], AF.Exp, [rsR], [rsR], scale=-0.5)
            return rs, rsR

        NBring = Ring([(pbank[6], bankR[6]), (pbank[7], bankR[7])])
        prenormed = set()
        sqRing = Ring([(scrF[i][:].bitcast(BF)[:, 0:BLK], scrFr.items[i][1]) for i in range(3)])

        def norm_groups(l, gi, b, db):
            st = {}

            def g_sq(c0):
                def f():
                    if c0 == 0:
                        if b >= 2:
                            flush()
                        st["ss"] = NBring.next()
                    ss, ssR = st["ss"]
                    sqs = []
                    for c in (c0, c0 + 1):
                        sq, sqR = sqRing.next()
                        act(sq, xT[:, c, bsl(b)], AF.Square, [xR[c][b]], [sqR])
                        sqs.append((c, sq, sqR))
                    for c, sq, sqR in sqs:
                        mm(ss[:], ones_bf, sq, c == 0, c == NCH - 1, [sqR, cbfR], [ssR])
                    if c0 + 2 == NCH:
                        st["rs"] = rstd_from(ss[:], ssR, D)
                return f

            def g_out(c0):
                def f():
                    rs, rsR = st["rs"]
                    for c in (c0, c0 + 1):
                        stt(hT[:, c, bsl(db)], xT[:, c, bsl(b)], gcol(gi, l, c), rs[:], ALU.mult, ALU.mult,
                            [xR[c][b], rsR, cstR], [hR[c][db]])
                return f

            return [g_sq(c0) for c0 in range(0, NCH, 2)] + [g_out(c0) for c0 in range(0, NCH, 2)]

        def norm_x(l, gi, blks, dst_blk0=0):
            for b in blks:
                if (l, gi, b) in prenormed:
                    prenormed.discard((l, gi, b))
                    continue
                for g in norm_groups(l, gi, b, b - dst_blk0):
                    g()
                drain(4)

        def proj_unit(key, src, srcR, blks, ring):
            wu, wR = W.get(key)
            outs = []
            for b in blks:
                pb, pR = ring.next()
                for k in range(NCH):
                    mm(pb[:], wu[:, k, :], src[:, k, bsl(b)], k == 0, k == NCH - 1, [wR, srcR[k][b]], [pR])
                outs.append((pb, pR, b))
                yield pb, pR, b
            W.done()

        qR = [Res(f"q{b}") for b in range(NBLK)]
        kR = [Res(f"k{b}") for b in range(NBLK)]
        memhR = [[qR[c // 2]] for c in range(NCH)]
        memT = u2hi[:, 12288:16384].bitcast(F32).rearrange("p (c m) -> p c m", c=NCH)
        mqT = u2hi[:, 8192:12288].rearrange("p (u t) -> p u t", u=2)
        memR = Res("memT")
        mqR = [[Res(f"mq{u}_{b}") for b in range(NBLK)] for u in range(2)]
        memhT = u2hi[:, 0:2048].rearrange("p (c m) -> p c m", c=NCH)
        mkT = sb("mkT_s", [128, 2, MEM], BF)
        mkR = [Res("mk0"), Res("mk1")]
        mvaug = sb("mvaug_s", [128, 2, 4, 128], BF)
        mvR = Res("mvaug")
        PTr = scrBr
        rl = sb("rl_s", [128, BLK], F32)
        rlR = Res("rl")

        S.add("dve", lambda e: e.memset(mvaug[:].rearrange("p a b c -> p (a b c)"), 1.0), writes=[mvR])

        def mem_kv(s, l):
            S.add("sp", lambda e, sem: [e.dma_start(out=memT[:, c, :], in_=memT_d[s, c]).then_inc(sem, 16) for c in range(NCH)],
                  reads=[TM], writes=[memR], dma=True, ndma=NCH)
            ss, ssR = Aring.next()
            for c in range(NCH):
                sq, sqR = scrBr.next()
                act(sq[:, 0:MEM], memT[:, c, :], AF.Square, [memR, TM], [sqR])
                mm(ss[:, 0:MEM], ones_bf, sq[:, 0:MEM], c == 0, c == NCH - 1, [sqR, cbfR], [ssR])
            rs, rsR = rstd_from(ss[:, 0:MEM], ssR, D, MEM)
            for c in range(NCH):
                stt(memhT[:, c, :], memT[:, c, :], gcol(4, l, c), rs[:, 0:MEM], ALU.mult, ALU.mult,
                    [memR, rsR, cstR, TM], [memhR[c][0]])
            for u in range(2):
                wu, wR = W.get(("mk", l, u))
                pb, pR = Pring.next()
                for k in range(NCH):
                    mm(pb[:, 0:MEM], wu[:, k, :], memhT[:, k, :], k == 0, k == NCH - 1, [wR, memhR[k][0]], [pR])
                act(mkT[:, u, :], pb[:, 0:MEM], AF.Copy, [pR], [mkR[u]])
                W.done()
            for u in range(2):
                wu, wR = W.get(("mv", l, u))
                for kt in range(2):
                    sp_, spR = smRing.next()
                    for k in range(NCH):
                        mm(sp_, memhT[:, k, kt * 128:(kt + 1) * 128], wu[:, k, :], k == 0, k == NCH - 1,
                           [wR, memhR[k][0]], [spR])
                    act(mvaug[:, kt, 2 * u, 0:64], sp_[:, 0:64], AF.Copy, [spR], [mvR])
                    cp(mvaug[:, kt, 2 * u + 1, 64:128], sp_[:, 64:128], [spR], [mvR])
                W.done()

        def mem_qattn(s, l):
            for u in range(2):
                for pb, pR, b in proj_unit(("mq", l, u), hT, hR, range(NBLK), Pring):
                    act(mqT[:, u, bsl(b)], pb[:], AF.Copy, [pR, TM], [mqR[u][b]])
            obs = {}
            pend = []

            def emit_pv(u, hh, qb, kt, pt, ptR):
                ob, oR = obs[(u, hh, qb)]
                mm(ob[:], mvaug[:, kt, 2 * u + hh, :], pt[:], kt == 0, kt == 1, [mvR, ptR], [oR])
                if kt == 1:
                    finish_attn(ob, oR, hh, 6 + u, qb)

            for u in range(2):
                for hh in range(2):
                    p0 = 64 * hh
                    for qb in range(NBLK):
                        for kt in range(2):
                            if kt == 0:
                                obs[(u, hh, qb)] = Oring.next()
                            sb_, sR = P4ring.next()
                            mm(sb_[:], mkT[p0:p0 + 64, u, kt * 128:(kt + 1) * 128], mqT[p0:p0 + 64, u, bsl(qb)], True, True,
                               [mkR[u], mqR[u][qb], TM], [sR])
                            pt, ptR = PTr.next()
                            act(pt[:], sb_[:], AF.Exp, [sR], [ptR], scale=0.125)
                            pend.append((u, hh, qb, kt, pt, ptR))
                            if len(pend) > 2:
                                emit_pv(*pend.pop(0))
            while pend:
                emit_pv(*pend.pop(0))

        def finish_attn(ob, oR, hh, chunk, qb):
            p0 = 64 * hh
            q0 = 64 - p0
            act(rl[p0:p0 + 64, :], ob[q0:q0 + 64, :], AF.Ln, [oR], [rlR])
            act(rl[p0:p0 + 64, :], rl[p0:p0 + 64, :], AF.Exp, [rlR], [rlR], scale=-1.0)
            tt(mixT[p0:p0 + 64, chunk, bsl(qb)], ob[p0:p0 + 64, :], rl[p0:p0 + 64, :], ALU.mult, [oR, rlR], [mR[chunk][qb]])

        cosT = u2hi[:, 8192:12288].bitcast(F32)
        sinT = u2hi[:, 12288:16384].bitcast(F32)
        tabC = [Res(f"tabC{b}") for b in range(NBLK)]
        tabS = [Res(f"tabS{b}") for b in range(NBLK)]
        perm_f = cst[:, C_PERM:C_PERM + 128]
        dbc = tab[:].bitcast(F32)
        posi = rl[:].bitcast(I32)
        posiR = rlR
        TWO_PI = 2.0 * math.pi
        CW1 = float(np.float32(6.28125))
        CW2 = float(np.float32(TWO_PI - 6.28125))

        def rotary_tables(s):
            for b in range(NBLK):
                S.add("sp", (lambda b_: lambda e, sem: e.dma_start(out=posi[:], in_=pos_d[s:s + 1, bsl(b_)].broadcast_to([128, BLK])).then_inc(sem, 16))(b),
                      writes=[posiR], dma=True)
                ang, angR = scrFr.next()
                cp(ang[:], posi[:], [posiR], [angR])
                ts(ang[:], ang[:], cst[:, C_INVF:C_INVF + 1], None, ALU.mult, None, [angR, cstR], [angR])
                kf, kfR = scrFr.next()
                ts(kf[:], ang[:], 1.0 / TWO_PI, None, ALU.mult, None, [angR], [kfR])
                cp(posi[:], kf[:], [kfR], [posiR])
                cp(kf[:], posi[:], [posiR], [kfR])
                stt(ang[:], kf[:], -CW1, ang[:], ALU.mult, ALU.add, [kfR, angR], [angR])
                stt(ang[:], kf[:], -CW2, ang[:], ALU.mult, ALU.add, [kfR, angR], [angR])
                for which, shift in ((0, 0.0), (1, math.pi / 2)):
                    y, yR = scrFr.next()
                    ts(y[:], ang[:], shift, None, ALU.add, None, [angR], [yR])
                    ts(kf[:], y[:], math.pi, -TWO_PI, ALU.is_gt, ALU.mult, [yR], [kfR])
                    tt(y[:], y[:], kf[:], ALU.add, [yR, kfR], [yR])
                    ts(kf[:], y[:], -math.pi, TWO_PI, ALU.is_lt, ALU.mult, [yR], [kfR])
                    tt(y[:], y[:], kf[:], ALU.add, [yR, kfR], [yR])
                    ts(y[:], y[:], 3.1415925, -3.1415925, ALU.min, ALU.max, [yR], [yR])
                    if which == 0:
                        act(sinT[:, bsl(b)], y[:], AF.Sin, [yR, TM], [tabS[b], memR])
                    else:
                        act(cosT[:, bsl(b)], y[:], AF.Sin, [yR, TM], [tabC[b], mqR[b // 2][2 * (b % 2)], mqR[b // 2][2 * (b % 2) + 1]])

        qTp = u2hi[:, 0:2048]
        kTr = u2hi[:, 2048:4096]
        ktok = u2hi[:, 4096:6144].rearrange("p (n c) -> p n c", c=128)
        vtok = u2hi[:, 6144:8192].rearrange("p (n c) -> p n c", c=128)
        ktokR = [Res(f"ktok{n}") for n in range(NT)]
        vtokR = [Res(f"vtok{n}") for n in range(NT)]
        AT = [smallb[:, i * 128:(i + 1) * 128] for i in range(2)]
        ATr = Ring([(AT[i], Res(f"AT{i}")) for i in range(2)])
        stbf = [smallb[:, 256 + i * 128:256 + (i + 1) * 128] for i in range(2)]
        stbfr = Ring([(stbf[i], Res(f"stbf{i}")) for i in range(2)])
        state = small[:, 0:128]
        stateR = Res("state")
        sgs = [(sb("sg0_s", [128, BLK], BF), Res("sg0")), (rl[:].bitcast(BF)[:, 0:BLK], rlR)]

        qf0 = small[:, 1024:1152]
        kf0 = small[:, 1152:1280]
        qf0R = Res("qf0")
        kf0R = Res("kf0")

        def rope_unit(key, l, dstT, dstR, decay_h):
            pend = None

            def tail(b, raw, rawR, t1, t1R):
                pp, ppR = Aring.next()
                mm(pp[:], perm_f, raw[:], True, True, [rawR, cstR], [ppR])
                t2, t2R = scrFr.next()
                tt(t2[:], pp[:], sinT[:, bsl(b)], ALU.mult, [ppR, tabS[b], TM], [t2R])
                if decay_h is None:
                    tt(dstT[:, bsl(b)], t1[:], t2[:], ALU.add, [t1R, t2R, TM], [dstR[b]])
                    if b == 0:
                        tt(kf0, t1[:, 0:128], t2[:, 0:128], ALU.add, [t1R, t2R], [kf0R], eng="pool")
                else:
                    tt(t1[:], t1[:], t2[:], ALU.add, [t1R, t2R], [t1R])
                    qd = qdb[:, decay_h:decay_h + 1, :].broadcast_to([128, 4, 128])
                    tt(dstT[:, bsl(b)].rearrange("p (n c) -> p n c", c=128), t1[:].rearrange("p (n c) -> p n c", c=128), qd,
                       ALU.mult, [t1R, cbfR, TM], [dstR[b]], eng="pool")
                    if b == 0:
                        tt(qf0, t1[:, 0:128], qdb[:, decay_h, :], ALU.mult, [t1R, cbfR], [qf0R], eng="pool")

            for pb, pR, b in proj_unit(key, hT, hR, range(NBLK), Pring):
                if pend is not None:
                    tail(*pend)
                raw, rawR = scrFr.next()
                act(raw[:], pb[:], AF.Copy, [pR], [rawR])
                t1, t1R = scrFr.next()
                tt(t1[:], pb[:], cosT[:, bsl(b)], ALU.mult, [pR, tabC[b], TM], [t1R])
                pend = (b, raw, rawR, t1, t1R)
            tail(*pend)

        def retention_layer(s, l):
            for h in range(6):
                gam = 1.0 - 2.0 ** (-5.0 - h)
                cd = gam ** 128
                rope_unit(("q", l, h), l, qTp, qR, h)
                rope_unit(("k", l, h), l, kTr, kR, None)
                for n in range(NT):
                    tp, tpR = trRing.next()
                    S.add("pe", (lambda tp_, n_: lambda e: e.transpose(out=tp_, in_=kTr[:, n_ * 128:(n_ + 1) * 128], identity=ident_bf))(tp, n),
                          reads=[kR[n // 4], cbfR, TM], writes=[tpR])
                    act(ktok[:, n, :], tp, AF.Copy, [tpR, cstR, TM], [ktokR[n]], scale=cst[:, C_KD + h:C_KD + h + 1])
                wu, wR = W.get(("v", l, h))
                for n in range(NT):
                    sp_, spR = smRing.next()
                    for k in range(NCH):
                        mm(sp_, hT[:, k, n * 128:(n + 1) * 128], wu[:, k, :], k == 0, k == NCH - 1, [wR, hR[k][n // 4]], [spR])
                    if n % 2 == 0:
                        act(vtok[:, n, :], sp_, AF.Copy, [spR, TM], [vtokR[n]])
                    else:
                        cp(vtok[:, n, :], sp_, [spR, TM], [vtokR[n]])
                W.done()
                gu, gR = W.get(("g", l, h))
                obs = {}
                sb_for = {}
                mask_h = cst[:, C_MASK + h * 128:C_MASK + (h + 1) * 128]

                def emit_o(n, at, atR):
                    b = n // 4
                    ob, oR = obs[b]
                    csl = slice((n % 4) * 128, (n % 4 + 1) * 128)
                    tsl = slice(n * 128, (n + 1) * 128)
                    mm(ob[:, csl], vtok[:, n, :], at, True, n == 0, [vtokR[n], atR, TM], [oR])
                    if n > 0:
                        sbf, sbfR = sb_for[n]
                        mm(ob[:, csl], sbf, qTp[:, tsl], False, True, [sbfR, qR[b], TM], [oR])
                    if n % 4 == 3:
                        rsq, rsqR = scrBr.next()
                        act(rsq[:], ob[:], AF.Square, [oR], [rsqR])
                        gnext = gate_pe(b + 1) if b + 1 < NBLK else None
                        ss, ssR = Aring.next()
                        mm(ss[:], ones_bf, rsq[:], True, True, [rsqR, cbfR], [ssR])
                        rs, rsR = rstd_from(ss[:], ssR, 128)
                        if gnext is not None:
                            gate_act(b + 1, *gnext)

                        def fin(b=b, ob=ob, oR=oR, rs=rs, rsR=rsR):
                            tmp, tmpR = scrFr.next()
                            tt(tmp[:], ob[:], rs[:], ALU.mult, [oR, rsR], [tmpR])
                            sgb, sgbR = sgs[b % 2]
                            tt(mixT[:, h, bsl(b)], tmp[:], sgb[:], ALU.mult, [tmpR, sgbR], [mR[h][b]], eng="pool")
                        late.append((n + 3, fin))

                def gate_pe(b):
                    gp, gpR = Pring.next()
                    for k in range(NCH):
                        mm(gp[:], gu[:, k, :], hT[:, k, bsl(b)], k == 0, k == NCH - 1, [gR, hR[k][b]], [gpR])
                    return gp, gpR

                def gate_act(b, gp, gpR):
                    e1, e1R = scrFr.next()
                    act(e1[:], gp[:], AF.Exp, [gpR], [e1R], scale=-1.0)
                    act(e1[:], e1[:], AF.Ln, [e1R, cstR], [e1R], bias=one_c, scale=1.0)
                    act(e1[:], e1[:], AF.Exp, [e1R], [e1R], scale=-1.0)
                    def fin(b=b, gp=gp, gpR=gpR, e1=e1, e1R=e1R):
                        sgb, sgbR = sgs[b % 2]
                        tt(sgb[:], gp[:], e1[:], ALU.mult, [gpR, e1R], [sgbR])
                    late.append((4 * b + 2, fin))

                late = []
                pend_o = None
                for n in range(NT):
                    b = n // 4
                    if n % 4 == 0:
                        obs[b] = Oring.next()
                        if b == 0:
                            gate_act(0, *gate_pe(0))
                    tsl = slice(n * 128, (n + 1) * 128)
                    sp_, spR = smRing.next()
                    if n == 0:
                        mm(sp_, kf0, qf0, True, True, [kf0R, qf0R], [spR])
                    else:
                        mm(sp_, kTr[:, tsl], qTp[:, tsl], True, True, [kR[b], qR[b], TM], [spR])
                    if n < NT - 1:
                        kv, kvR = smRing.next()
                        mm(kv, ktok[:, n, :], vtok[:, n, :], True, True, [ktokR[n], vtokR[n], TM], [kvR])
                    at, atR = ATr.next()
                    tt(at, sp_, mask_h, ALU.mult, [spR, cstR], [atR])
                    if pend_o is not None:
                        emit_o(*pend_o)
                    pend_o = (n, at, atR)
                    if n < NT - 1:
                        if n == 0:
                            cp(state, kv, [kvR], [stateR])
                        else:
                            stt(state, state, cd, kv, ALU.mult, ALU.add, [stateR, kvR], [stateR])
                        sbf, sbfR = stbfr.next()
                        cp(sbf, state, [stateR], [sbfR])
                        sb_for[n + 1] = (sbf, sbfR)
                    while late and late[0][0] <= n:
                        late.pop(0)[1]()
                emit_o(*pend_o)
                while late:
                    late.pop(0)[1]()
                W.done()

        vaug = u2hi[:, 4096:8192].rearrange("p (t h e) -> p t h e", t=NT, h=2)
        vaugR = [Res(f"vaug{n}") for n in range(NT)]
        nlogf = small[:, 128:128 + 192].rearrange("p (t h) -> p t h", h=12)
        carry = small[:, 320:320 + 192].rearrange("p (t h) -> p t h", h=12)
        dcol = small[:, 512:512 + 192].rearrange("p (t h) -> p t h", h=12)
        sprev = small[:, 704:704 + 192].rearrange("p (t h) -> p t h", h=12)
        zt = small[:, 960:972]
        ztR = Res("zt")
        nlogfR = Res("nlogf")
        carryR = Res("carry")
        dcolR = Res("dcol")
        sprevR = Res("sprev")
        drow = tab[0:12, 0:2048]
        drowR = Res("drow")
        dqsem = [Res("dqA"), Res("dqB")]
        qX = [u2hi[:, 0:2048], u2hi[:, 8192:10240]]
        kX = [u2hi[:, 2048:4096], u2hi[:, 10240:12288]]
        qXR = [qR, mqR[0]]
        kXR = [kR, mqR[1]]

        def fox_layer(s, l):
            S.add("dve", lambda e: e.memset(u2hi[:, 4096:8192], 1.0), reads=[TM], writes=vaugR)
            for hh in range(2):
                S.add("dve", (lambda hh_: lambda e: e.memset(kX[hh_][64:128, :], 0.0))(hh), reads=[TM], writes=kXR[hh])
                S.add("dve", (lambda hh_: lambda e: e.memset(kX[hh_][64:65, :], 1.0))(hh), reads=[TM], writes=kXR[hh])
                S.add("dve", (lambda hh_: lambda e: e.memset(qX[hh_][64:128, :], 0.0))(hh), reads=[TM], writes=qXR[hh])
            nl_flat = small[:, 128:320]
            ca_flat = small[:, 320:512]
            dc_flat = small[:, 512:704]
            fb, fbR = Aring.next()
            wu, wR = W.get(("f", l, 0))
            fl = fb[:, 0:192].rearrange("p (t h) -> p t h", h=12)
            for n in range(NT):
                for k in range(NCH):
                    mm(fl[:, n, :], hT[:, k, n * 128:(n + 1) * 128], wu[:, k, 0:12], k == 0, k == NCH - 1,
                       [wR, hR[k][n // 4]], [fbR])
            W.done()
            bfb = cst[:, C_BF:C_BF + 12].rearrange("p (o h) -> p o h", o=1).broadcast_to([128, NT, 12])
            tt(nlogf, fl, bfb, ALU.add, [fbR, cstR], [nlogfR])
            act(nl_flat, nl_flat, AF.Exp, [nlogfR], [nlogfR], scale=-1.0)
            act(nl_flat, nl_flat, AF.Ln, [nlogfR, cstR], [nlogfR], bias=one_c, scale=1.0)
            tb, tbR = Aring.next()
            mm(tb[:, 0:192], ones_f, nl_flat, True, True, [cstR, nlogfR], [tbR])
            db, dbR = Aring.next()
            mm(db[:, 0:192], U_f, nl_flat, True, True, [cstR, nlogfR], [dbR])
            S.add("dve", lambda e: e.memset(carry[:, 0, :], 0.0), writes=[carryR])
            S.add("dve", lambda e: e.memset(sprev[:, 0, :], 0.0), writes=[sprevR])
            for n in range(NT - 1):
                tt(carry[:, n + 1, :], carry[:, n, :], tb[:, n * 12:(n + 1) * 12], ALU.add, [tbR, carryR], [carryR])
                tt(sprev[:, n + 1, :], sprev[:, n, :], nlogf[:, n, :], ALU.add, [sprevR, nlogfR], [sprevR], eng="pool")
            tt(dc_flat, db[:, 0:192], ca_flat, ALU.add, [dbR, carryR], [dcolR])
            for g in range(NBLK):
                rb, rbR = P4ring.next()
                for t in range(4):
                    n = 4 * g + t
                    mm(rb[0:12, t * 128:(t + 1) * 128], nlogf[:, n, :], U_f, True, n == 0, [cstR, nlogfR], [rbR])
                    if n > 0:
                        mm(rb[0:12, t * 128:(t + 1) * 128], sprev[:, n, :], ones_f, False, True, [cstR, sprevR], [rbR])
                act(drow[:, bsl(g)], rb[0:12, :], AF.Copy, [rbR], [drowR], scale=-8.0)
            for p in range(6):
                for hh in range(2):
                    S.add("sp", (lambda hh_, h_: lambda e, sem: e.dma_start(out=qX[hh_][64:65, :], in_=tab[h_:h_ + 1, 0:2048]).then_inc(sem, 16))(hh, 2 * p + hh),
                          reads=[drowR, TM], writes=qXR[hh], dma=True, dres=dqsem[hh])
                for pb, pR, b in proj_unit(("q", l, p), hT, hR, range(NBLK), Pring):
                    cp(qX[0][0:64, bsl(b)], pb[0:64, :], [pR, TM], [qXR[0][b]])
                    cp(qX[1][0:64, bsl(b)], pb[64:128, :], [pR, TM], [qXR[1][b]])
                for pb, pR, b in proj_unit(("k", l, p), hT, hR, range(NBLK), Pring):
                    cp(kX[0][0:64, bsl(b)], pb[0:64, :], [pR, TM], [kXR[0][b]])
                    cp(kX[1][0:64, bsl(b)], pb[64:128, :], [pR, TM], [kXR[1][b]])
                wu, wR = W.get(("v", l, p))
                for n in range(NT):
                    sp_, spR = smRing.next()
                    for k in range(NCH):
                        mm(sp_, hT[:, k, n * 128:(n + 1) * 128], wu[:, k, :], k == 0, k == NCH - 1, [wR, hR[k][n // 4]], [spR])
                    cp(vaug[:, n, 0, 0:64], sp_[:, 0:64], [spR, TM], [vaugR[n]])
                    cp(vaug[:, n, 1, 64:128], sp_[:, 64:128], [spR, TM], [vaugR[n]])
                W.done()
                steps = [(hh, qb, kt) for hh in range(2) for qb in range(NBLK) for kt in range(4 * qb + 4)]
                obs = {}
                pend = []

                def emit_pv(hh, qb, kt, pt, ptR, c0, w):
                    ob, oR = obs[(hh, qb)]
                    nkt = 4 * qb + 4
                    mm(ob[:, c0:BLK], vaug[:, kt, hh, :], pt[:, 0:w], kt == 0, kt == nkt - 1, [vaugR[kt], ptR, TM], [oR])
                    if kt == nkt - 1:
                        finish_attn(ob, oR, hh, p, qb)

                for (hh, qb, kt) in steps:
                    h = 2 * p + hh
                    if kt == 0:
                        obs[(hh, qb)] = Oring.next()
                    r = max(0, kt - 4 * qb)
                    c0 = 128 * r
                    w = BLK - c0
                    sb_, sR = P4ring.next()
                    mm(sb_[:, 0:w], kX[hh][:, kt * 128:(kt + 1) * 128], qX[hh][:, qb * BLK + c0:(qb + 1) * BLK],
                       True, kt < 4 * qb, [kXR[hh][kt // 4], qXR[hh][qb], TM], [sR])
                    if kt >= 4 * qb:
                        mm(sb_[:, 0:128], ident_bf, neg_bf, False, True, [cbfR], [sR])
                    pt, ptR = PTr.next()
                    act(pt[:, 0:w], sb_[:, 0:w], AF.Exp, [sR, dcolR], [ptR], bias=dcol[:, kt, h:h + 1], scale=0.125)
                    pend.append((hh, qb, kt, pt, ptR, c0, w))
                    if len(pend) > 2:
                        emit_pv(*pend.pop(0))
                while pend:
                    emit_pv(*pend.pop(0))

        deferred = []

        def drain(k=1):
            for _ in range(k):
                if deferred:
                    deferred.pop(0)()

        def flush():
            while deferred:
                deferred.pop(0)()

        def rstd_inplace(bank, bR, n):
            act(bank[:], bank[:], AF.Ln, [bR, cstR], [bR], bias=eps_c, scale=1.0 / n)
            act(bank[:], bank[:], AF.Exp, [bR], [bR], scale=-0.5)

        def tail_pair(stage, stageR, gc, bank, bR, xap, xRes, oc):
            def f():
                t, tR = scrFr.next()
                stt(t[:], stage, gc, bank[:], ALU.mult, ALU.mult, [stageR, bR, cstR], [tR])
                tt(xap, xap, t[:], ALU.add, [xRes, tR], [xRes], eng=("pool" if oc % 2 else "dve"))
            return f

        def out_proj(l):
            ssb = [(pbank[2 + b], bankR[2 + b]) for b in range(NBLK)]
            pend = None
            for oc in range(NCH):
                for pb, pR, b in proj_unit(("out", l, oc), mixT, mR, range(NBLK), Pring):
                    cp(hT[:, oc, bsl(b)], pb[:], [pR], [hR[oc][b]])
                    sq, sqR = scrBr.next()
                    act(sq[:], pb[:], AF.Square, [pR], [sqR])
                    if pend is not None:
                        mm(*pend)
                    pend = (ssb[b][0][:], ones_bf, sq[:], oc == 0, oc == NCH - 1, [sqR, cbfR], [ssb[b][1]])
            mm(*pend)
            for b in range(NBLK):
                rstd_inplace(ssb[b][0], ssb[b][1], D)
            for b in range(NBLK):
                for oc in range(NCH):
                    f = tail_pair(hT[:, oc, bsl(b)], hR[oc][b], gcol(1, l, oc), ssb[b][0], ssb[b][1], xT[:, oc, bsl(b)], xR[oc][b], oc)
                    if b < 2:
                        f()
                    else:
                        deferred.append(f)

        def u2T(f, hb):
            if f < 16:
                return mixT[:].rearrange("p c t -> p (c t)")[:, f * 1024 + hb * BLK:f * 1024 + (hb + 1) * BLK]
            return u2hi[:, (f - 16) * 1024 + hb * BLK:(f - 16) * 1024 + (hb + 1) * BLK]

        u2R = [[Res(f"u2_{f}_{hb}") for hb in range(2)] for f in range(32)]
        for f in range(16):
            for hb in range(2):
                u2R[f][hb] = mR[f // 2][(f % 2) * 2 + hb]

        def mlp(l, after_half0=None, next_layer=None):
            S.add("dve", lambda e: e.memset(junk[:], 0.0), writes=[TM, junkR])
            for half in range(2):
                blks = [2 * half, 2 * half + 1]
                norm_x(l, 2, blks, dst_blk0=2 * half)
                hRl = [[hR[c][0], hR[c][1]] for c in range(NCH)]
                for f in range(32):
                    tmr = [TM] if f >= 16 else []
                    for pb, pR, b in proj_unit(("up", l, f), hT, hRl, range(2), P4ring):
                        r, rR = scrBr.next()
                        act(r[:], pb[:], AF.Relu, [pR], [rR])
                        tt(u2T(f, b), r[:], r[:], ALU.mult, [rR] + tmr, [u2R[f][b]])
                    drain(1)
                flush()
                if half == 1 and after_half0 is not None:
                    after_half0()
                ssb = [(pbank[4 + b], bankR[4 + b]) for b in range(2)]
                pend = None
                early = []
                if half == 0 and EARLY:
                    for b in (2, 3):
                        early += norm_groups(l, 2, b, b - 2)
                        prenormed.add((l, 2, b))
                elif next_layer is not None and EARLY:
                    for b in (0, 1):
                        early += norm_groups(next_layer, 0, b, b)
                        prenormed.add((next_layer, 0, b))
                for oc in range(NCH):
                    acc = [P4ring.next() for _ in range(2)]
                    for g in range(4):
                        if oc >= 1 and early:
                            early.pop(0)()
                        wu, wR = W.get(("down", l, oc * 4 + g))
                        for k in range(NCH):
                            f = g * 8 + k
                            tmr = [TM] if f >= 16 else []
                            for b in range(2):
                                mm(acc[b][0][:], wu[:, k, :], u2T(f, b), f == 0, f == 31, [wR, u2R[f][b]] + tmr, [acc[b][1]])
                        W.done()
                    for b in range(2):
                        cp(hT[:, oc, bsl(2 + b)], acc[b][0][:], [acc[b][1]], [hR[oc][2 + b]])
                        sq, sqR = scrBr.next()
                        act(sq[:], acc[b][0][:], AF.Square, [acc[b][1]], [sqR])
                        if pend is not None:
                            mm(*pend)
                        pend = (ssb[b][0][:], ones_bf, sq[:], oc == 0, oc == NCH - 1, [sqR, cbfR], [ssb[b][1]])
                mm(*pend)
                while early:
                    early.pop(0)()
                for b in range(2):
                    rstd_inplace(ssb[b][0], ssb[b][1], D)
                for b in range(2):
                    xb = 2 * half + b
                    for oc in range(NCH):
                        deferred.append(tail_pair(hT[:, oc, bsl(2 + b)], hR[oc][2 + b], gcol(3, l, oc), ssb[b][0], ssb[b][1],
                                                  xT[:, oc, bsl(xb)], xR[oc][xb], oc))
            S.add("dve", lambda e: e.memset(junk[:], 0.0), writes=[TM, junkR])

        xsem = [Res(f"xsem{b}") for b in range(NBLK)]

        def load_x(s, b):
            S.add("sp", lambda e, sem: e.dma_start(out=xT[:, :, bsl(b)], in_=xT_d[s][:, :, bsl(b)].rearrange("c p t -> p c t")).then_inc(sem, 16),
                  writes=[xR[c][b] for c in range(NCH)], dma=True, dres=xsem[b])

        def store_x(s, b):
            S.add("sp", lambda e, sem: e.dma_start(out=out_d[s][:, :, bsl(b)].rearrange("c p t -> p c t"), in_=xT[:, :, bsl(b)]).then_inc(sem, 16),
                  reads=[xR[c][b] for c in range(NCH)], dma=True, dres=xsem[b], final=True)

        xstate = {"early": None}

        def early_io(s):
            for b in range(2):
                store_x(s, b)
            if s + 1 < n_seq:
                for b in range(2):
                    load_x(s + 1, b)
            xstate["early"] = s
        for s in range(n_seq):
            for b in range(NBLK):
                if not (s > 0 and b < 2):
                    load_x(s, b)
            for l in layers:
                if stop >= 3:
                    mem_kv(s, l)
                if stop >= 2:
                    norm_x(l, 0, range(NBLK))
                if stop >= 3:
                    mem_qattn(s, l)
                if l % 2 == 0 and stop >= 1:
                    rotary_tables(s)
                if stop >= 4:
                    if l % 2 == 0:
                        retention_layer(s, l)
                    else:
                        fox_layer(s, l)
                if stop >= 5:
                    out_proj(l)
                if stop >= 6:
                    last = (l == layers[-1])
                    mlp(l, (lambda s_=s: early_io(s_)) if last else None, None if last else layers[layers.index(l) + 1])
            flush()
            for b in range(NBLK):
                if not (xstate["early"] == s and b < 2):
                    store_x(s, b)
        assert stop < 6 or W.consumed == len(plan), (W.consumed, len(plan))
        S.emit(st)
    return nc


_CACHE = {}


def kernel(**inp):
    inp = {k: np.asarray(v) for k, v in inp.items()}
    n_cores = 8
    n_seq = 2
    layers = (0, 1)
    wall, widx = pack_wall(inp, layers)
    cst, cin = build_consts(inp)
    key = ("prog", n_seq, layers, wall.shape[0])
    if key not in _CACHE:
        _CACHE[key] = build_program(n_seq, layers, widx, wall.shape[0])
    nc = _CACHE[key]
    x = inp["x"].astype(np.float32, copy=False)
    mem = inp["mem"].astype(np.float32, copy=False)
    pos = inp["positions"].astype(np.int32, copy=False)
    in_maps = []
    for c in range(n_cores):
        sl = slice(c * n_seq, (c + 1) * n_seq)
        xT = np.ascontiguousarray(x[sl].transpose(0, 2, 1)).reshape(n_seq, NCH, 128, SEQ)
        mT = np.ascontiguousarray(mem[sl].transpose(0, 2, 1)).reshape(n_seq, NCH, 128, MEM)
        in_maps.append({"xT": xT, "memT": mT, "pos": np.ascontiguousarray(pos[sl]), "wall": wall, "cst": cst, "cin": cin})
    res = run_bass_kernel_spmd(nc, in_maps, core_ids=list(range(n_cores)))
    outs = []
    for c in range(n_cores):
        o = np.asarray(res.results[c]["outT"]).reshape(n_seq, D, SEQ).transpose(0, 2, 1)
        outs.append(o)
    return np.ascontiguousarray(np.concatenate(outs, axis=0)).astype(np.float32, copy=False)
```

```python
import math
from contextlib import ExitStack

import numpy as np
import concourse.bass as bass
import concourse.mybir as mybir
from concourse.bass_utils import run_bass_kernel_spmd

F32 = mybir.dt.float32
BF = mybir.dt.bfloat16
I32 = mybir.dt.int32
AF = mybir.ActivationFunctionType
ALU = mybir.AluOpType

D = 1024
SEQ = 2048
NCH = 8
NBLK = 4
BLK = 512
NT = 16
MEM = 256
EPS = 1e-6
NSLOT = 3
EARLY = True
STRICT = True
NEG = -240000.0

C_MASK = 0
C_KD = C_MASK + 768
C_U = C_KD + 6
C_ONES = C_U + 128
C_INVF = C_ONES + 128
C_EPS = C_INVF + 1
C_G = C_EPS + 1
C_BF = C_G + 80
C_PERM = C_BF + 12
NCST = C_PERM + 128
I_QD = 0
I_IDENT = 768
I_PERM = I_IDENT + 128
I_NEG = I_PERM + 128
I_ONES = I_NEG + 128
NINIT = I_ONES + 128
GAINS = ("g_pre_mix", "g_post_mix", "g_pre_mlp", "g_post_mlp", "g_mem")


class Res:
    __slots__ = ("name", "last_w", "readers", "dsem", "dcount", "excl", "tw", "trd")

    def __init__(self, name, excl=False):
        self.name = name
        self.excl = excl
        self.tw = None
        self.trd = []
        self.last_w = None
        self.readers = []
        self.dsem = None
        self.dcount = 0


class Op:
    __slots__ = ("eng", "fn", "deps", "signal", "tick", "is_dma", "dres", "dcount", "seq")


ENGS = ("pe", "act", "dve", "pool", "sp")


class Sched:
    def __init__(self, nc):
        self.nc = nc
        self.ops = []
        self.by_eng = {e: [] for e in ENGS}
        self.out_dmas = []

    def add(self, eng, fn, reads=(), writes=(), dma=False, ndma=1, dres=None, final=False):
        op = Op()
        op.eng = eng
        op.fn = fn
        op.signal = False
        op.tick = None
        op.is_dma = dma
        op.dres = None
        op.dcount = None
        op.seq = len(self.ops)
        raw_src = list(reads)
        true_w = list(writes)
        if any(r.excl for r in reads):
            writes = list(writes) + [r for r in reads if r.excl]
            reads = [r for r in reads if not r.excl]
        deps = set()
        for r in reads:
            if r.last_w is not None:
                deps.add(r.last_w)
        for r in writes:
            if r.last_w is not None:
                deps.add(r.last_w)
            for rd in r.readers:
                deps.add(rd)
        keep = []
        for d in deps:
            if d is op:
                continue
            if (not d.is_dma) and d.eng == eng and not dma:
                if eng == "pe":
                    continue
                if STRICT:
                    hz = any(r.tw is d for r in raw_src) or any(r.tw is d for r in true_w) or \
                        any(d in r.trd for r in true_w)
                    if not hz:
                        continue
                elif not any(r.last_w is d for r in raw_src):
                    continue
            keep.append(d)
        best = {}
        for d in keep:
            if d.is_dma:
                k_ = ("d", id(d.dres))
                if k_ not in best or best[k_].dcount < d.dcount:
                    best[k_] = d
            else:
                k_ = ("e", d.eng)
                if k_ not in best or best[k_].seq < d.seq:
                    best[k_] = d
        keep = list(best.values())
        op.deps = keep
        for d in keep:
            d.signal = True
        if dma:
            if dres is None:
                dres = writes[0] if writes else reads[0]
            op.dres = dres
            dres.dcount += ndma
            op.dcount = dres.dcount
        for r in reads:
            r.readers.append(op)
        for r in writes:
            r.last_w = op
            r.readers = []
        for r in raw_src:
            r.trd.append(op)
        for r in true_w:
            r.tw = op
            r.trd = []
        self.ops.append(op)
        self.by_eng[eng].append(op)
        if final:
            self.out_dmas.append(op)
        return op

    def emit(self, stack):
        nc = self.nc
        esem = {e: stack.enter_context(nc.semaphore("s_" + e)) for e in ENGS}
        for op in self.ops:
            if op.is_dma and op.dres.dsem is None:
                op.dres.dsem = stack.enter_context(nc.semaphore("d_" + op.dres.name))
        for e in ENGS:
            t = 0
            for op in self.by_eng[e]:
                if (not op.is_dma) and op.signal:
                    t += 1
                    op.tick = t
        block = stack.enter_context(nc.Block())

        def run_engine(e, engobj):
            waited = {}
            for op in self.by_eng[e]:
                need = {}
                for d in op.deps:
                    if d.is_dma:
                        key, val, sem = id(d.dres), 16 * d.dcount, d.dres.dsem
                    else:
                        key, val, sem = d.eng, d.tick, esem[d.eng]
                    if need.get(key, (0, None))[0] < val:
                        need[key] = (val, sem)
                for key, (val, sem) in need.items():
                    if waited.get(key, 0) >= val:
                        continue
                    engobj.wait_ge(sem, val)
                    waited[key] = val
                if op.is_dma:
                    op.fn(engobj, op.dres.dsem)
                else:
                    ins = op.fn(engobj)
                    if op.signal:
                        ins.then_inc(esem[e], 1)
            if e == "sp":
                for op in self.out_dmas:
                    engobj.wait_ge(op.dres.dsem, 16 * op.dres.dcount)

        def mk(e):
            return lambda engobj: run_engine(e, engobj)

        block.tensor(mk("pe"))
        block.scalar(mk("act"))
        block.vector(mk("dve"))
        block.gpsimd(mk("pool"))
        block.sync(mk("sp"))


class Ring:
    def __init__(self, items):
        self.items = items
        self.i = 0

    def next(self):
        it = self.items[self.i % len(self.items)]
        self.i += 1
        return it


def layer_units(l):
    u = [("mk", l, 0), ("mk", l, 1), ("mv", l, 0), ("mv", l, 1), ("mq", l, 0), ("mq", l, 1)]
    if l % 2 == 0:
        for h in range(6):
            u += [("q", l, h), ("k", l, h), ("v", l, h), ("g", l, h)]
    else:
        u += [("f", l, 0)]
        for p in range(6):
            u += [("q", l, p), ("k", l, p), ("v", l, p)]
    u += [("out", l, oc) for oc in range(8)]
    for half in range(2):
        u += [("up", l, f) for f in range(32)]
        u += [("down", l, oc * 4 + g) for oc in range(8) for g in range(4)]
    return u


def _unit(wm, c):
    blk = wm[:, c * 128:(c + 1) * 128]
    return np.ascontiguousarray(blk.reshape(8, 128, 128).transpose(1, 0, 2)).reshape(128, 1024)


def pack_wall(inp, layers):
    idx = {}
    units = []

    def put(key, arr):
        if key not in idx:
            idx[key] = len(units)
            units.append(arr)

    for l in layers:
        j = l // 2
        win = inp["w_in_ret"][j] if l % 2 == 0 else inp["w_in_fox"][j]
        mqo = 3072 if l % 2 == 0 else 2316
        for key in layer_units(l):
            if key in idx:
                continue
            name, _, i = key
            if name == "mk":
                a = _unit(inp["w_mem_kv"][l][:, 0:256], i)
            elif name == "mv":
                a = _unit(inp["w_mem_kv"][l][:, 256:512], i)
            elif name == "mq":
                a = _unit(win[:, mqo:mqo + 256], i)
            elif name == "q":
                a = _unit(win[:, 0:768], i)
            elif name == "k":
                a = _unit(win[:, 768:1536], i)
            elif name == "v":
                a = _unit(win[:, 1536:2304], i)
            elif name == "g":
                a = _unit(win[:, 2304:3072], i)
            elif name == "f":
                pad = np.zeros((1024, 128), np.float32)
                pad[:, 0:12] = win[:, 2304:2316]
                a = _unit(pad, 0)
            elif name == "out":
                a = _unit(inp["w_out"][l], i)
            elif name == "up":
                a = _unit(inp["w_up"][l], i)
            elif name == "down":
                oc, g = i // 4, i % 4
                a = _unit(inp["w_down"][l][g * 1024:(g + 1) * 1024, :], oc)
            put(key, a)
    wall = np.stack(units, axis=0).astype(np.float32)
    return wall, idx


def build_consts(inp):
    c = np.zeros((128, NCST), np.float32)
    ci = np.zeros((128, NINIT), np.float32)
    j = np.arange(128, dtype=np.float64)
    for h in range(6):
        lg = math.log(1.0 - 2.0 ** (-5.0 - h))
        m = (128.0 ** -0.5) * np.exp(-(j[:, None] + 1.0) * lg) * (j[:, None] <= j[None, :])
        c[:, C_MASK + h * 128:C_MASK + (h + 1) * 128] = m
        ci[:, I_QD + h * 128:I_QD + (h + 1) * 128] = np.exp((j[None, :] + 1.0) * lg)
        c[:, C_KD + h] = (128.0 ** -0.5) * np.exp((127.0 - j) * lg)
    c[:, C_U:C_U + 128] = (j[:, None] <= j[None, :])
    c[:, C_ONES:C_ONES + 128] = 1.0
    ci[:, I_IDENT:I_IDENT + 128] = np.eye(128)
    ci[:, I_ONES:I_ONES + 128] = 1.0
    perm = np.zeros((128, 128))
    for dp in range(64):
        perm[dp + 64, dp] = -1.0
        perm[dp, dp + 64] = 1.0
    ci[:, I_PERM:I_PERM + 128] = perm
    c[:, C_PERM:C_PERM + 128] = perm
    ci[:, I_NEG:I_NEG + 128] = np.where(j[:, None] > j[None, :], NEG, 0.0)
    invf = (10000.0 ** (-np.arange(0, 128, 2, dtype=np.float32) / np.float32(128))).astype(np.float32)
    c[:, C_INVF] = np.concatenate([invf, invf])
    c[:, C_EPS] = EPS
    for gi, g in enumerate(GAINS):
        for l in range(2):
            c[:, C_G + (gi * 2 + l) * 8:C_G + (gi * 2 + l + 1) * 8] = inp[g][l].reshape(8, 128).T
    c[:, C_BF:C_BF + 12] = np.broadcast_to(inp["b_forget"][0][None, :], (128, 12))
    return c, ci


def build_program(n_seq, layers, wall_idx, n_units, stop=99):
    nc = bass.Bass("TRN2", target_bir_lowering=False)
    xT_d = nc.dram_tensor("xT", [n_seq, NCH, 128, SEQ], F32, kind="ExternalInput").ap()
    memT_d = nc.dram_tensor("memT", [n_seq, NCH, 128, MEM], F32, kind="ExternalInput").ap()
    pos_d = nc.dram_tensor("pos", [n_seq, SEQ], I32, kind="ExternalInput").ap()
    wall_d = nc.dram_tensor("wall", [n_units, 128, 1024], F32, kind="ExternalInput").ap()
    cst_d = nc.dram_tensor("cst", [128, NCST], F32, kind="ExternalInput").ap()
    cin_d = nc.dram_tensor("cin", [128, NINIT], F32, kind="ExternalInput").ap()
    out_d = nc.dram_tensor("outT", [n_seq, NCH, 128, SEQ], F32, kind="ExternalOutput").ap()

    plan = []
    for s in range(n_seq):
        for l in layers:
            plan += [wall_idx[k] for k in layer_units(l)]

    with ExitStack() as st:
        S = Sched(nc)

        def sb(name, shape, dt):
            return st.enter_context(nc.sbuf_tensor(name, shape, dt))

        def ps(name, shape, dt):
            return st.enter_context(nc.psum_tensor(name, shape, dt))

        xT = sb("xT_s", [128, NCH, SEQ], F32)
        hT = sb("hT_s", [128, NCH, SEQ], BF)
        mixT = sb("mixT_s", [128, NCH, SEQ], BF)
        u2hi = sb("u2hi_s", [128, 16384], BF)
        tab = sb("tab_s", [128, 4096], BF)
        wr = sb("wr_s", [128, NSLOT, 1024], BF)
        cst = sb("cst_s", [128, NCST], F32)
        cbf = sb("cbf_s", [128, 4, 128], BF)
        qdb = sb("qdb_s", [128, 6, 128], BF)
        scrF = [sb(f"scrF{i}", [128, BLK], F32) for i in range(3)]
        scrB = [sb(f"scrB{i}", [128, BLK], BF) for i in range(4)]
        rsd = [sb(f"rsd{i}", [128, BLK], F32) for i in range(2)]
        small = sb("small_s", [128, 1280], F32)
        smallb = sb("smallb_s", [128, 512], BF)
        junk = sb("junk_s", [128, 2], F32)

        pbank = [ps(f"pb{i}", [128, BLK], F32) for i in range(8)]
        ptr = pbank[7][:].bitcast(BF)
        bankR = [Res(f"bank{i}", excl=True) for i in range(8)]
        smP = [(pbank[6][:, 0:128], bankR[6]), (pbank[7][:, 0:128], bankR[7])]
        trP = [(ptr[:, 0:128], bankR[7]), (pbank[6][:, 0:64].bitcast(BF), bankR[6])]
        smRing = Ring(smP)
        trRing = Ring(trP)
        Pring = Ring([(pbank[0], bankR[0]), (pbank[1], bankR[1])])
        Aring = Ring([(pbank[2], bankR[2]), (pbank[3], bankR[3])])
        Oring = Ring([(pbank[4], bankR[4]), (pbank[5], bankR[5])])
        P4ring = Ring([(pbank[i], bankR[i]) for i in range(4)])

        xR = [[Res(f"x{c}_{b}") for b in range(NBLK)] for c in range(NCH)]
        hR = [[Res(f"h{c}_{b}") for b in range(NBLK)] for c in range(NCH)]
        mR = [[Res(f"m{c}_{b}") for b in range(NBLK)] for c in range(NCH)]
        tabR = [Res(f"tab{i}") for i in range(4)]
        wrR = [Res(f"wr{i}") for i in range(NSLOT)]
        cstR = Res("cst")
        cbfR = Res("cbf")
        TM = Res("TM")
        scrFr = Ring([(scrF[i], Res(f"scrF{i}")) for i in range(3)])
        scrBr = Ring([(scrB[i], Res(f"scrB{i}")) for i in range(4)])
        rsdr = Ring([(rsd[i], Res(f"rsd{i}")) for i in range(2)])
        junkR = Res("junk")

        ones_bf = cbf[:, 0, :]
        ident_bf = cbf[:, 1, :]
        perm_bf = cbf[:, 2, :]
        neg_bf = cbf[:, 3, :]
        ones_f = cst[:, C_ONES:C_ONES + 128]
        U_f = cst[:, C_U:C_U + 128]
        eps_c = cst[:, C_EPS:C_EPS + 1]
        one_c = cst[:, C_ONES:C_ONES + 1]

        def gcol(gi, l, c):
            o = C_G + (gi * 2 + l) * 8 + c
            return cst[:, o:o + 1]

        class WRing:
            def __init__(self):
                self.issued = 0
                self.consumed = 0
                self.prefetch()

            def prefetch(self):
                while self.issued < len(plan) and self.issued < self.consumed + NSLOT:
                    i = self.issued
                    slot = i % NSLOT
                    src = wall_d[plan[i]]
                    dst = wr[:, slot, :]
                    S.add("pool", (lambda d_, s_: lambda e, sem: e.dma_start(out=d_, in_=s_).then_inc(sem, 16))(dst, src),
                          writes=[wrR[slot]], dma=True)
                    self.issued += 1

            def get(self, key):
                i = self.consumed
                assert plan[i] == wall_idx[key], (key, i)
                assert self.issued > i
                slot = i % NSLOT
                self.consumed += 1
                return wr[:, slot, :].rearrange("p (k m) -> p k m", k=8), wrR[slot]

            def done(self):
                self.prefetch()

        def mm(out, lhsT, rhs, start, stop, reads, writes):
            S.add("pe", lambda e: e.matmul(out, lhsT=lhsT, rhs=rhs, start=start, stop=stop), reads=reads, writes=writes)

        def act(out, in_, func, reads, writes, bias=None, scale=None):
            kw = {}
            if bias is not None:
                kw["bias"] = bias
            if scale is not None:
                kw["scale"] = scale
            S.add("act", lambda e: e.activation(out=out, in_=in_, func=func, **kw), reads=reads, writes=writes)

        def dve(fn, reads, writes):
            S.add("dve", fn, reads=reads, writes=writes)

        def tt(out, in0, in1, op, reads, writes, eng="dve"):
            S.add(eng, lambda e: e.tensor_tensor(out=out, in0=in0, in1=in1, op=op), reads=reads, writes=writes)

        def stt(out, in0, scalar, in1, op0, op1, reads, writes, eng="dve"):
            S.add(eng, lambda e: e.scalar_tensor_tensor(out=out, in0=in0, scalar=scalar, in1=in1, op0=op0, op1=op1),
                  reads=reads, writes=writes)

        def ts(out, in0, s1, s2, op0, op1, reads, writes, eng="dve"):
            if s2 is None:
                S.add(eng, lambda e: e.tensor_scalar(out=out, in0=in0, scalar1=s1, scalar2=None, op0=op0), reads=reads, writes=writes)
            else:
                S.add(eng, lambda e: e.tensor_scalar(out=out, in0=in0, scalar1=s1, scalar2=s2, op0=op0, op1=op1),
                      reads=reads, writes=writes)

        def cp(out, in_, reads, writes, eng="dve"):
            S.add(eng, lambda e: e.tensor_copy(out=out, in_=in_), reads=reads, writes=writes)

        def recip(out, in_, reads, writes):
            S.add("dve", lambda e: e.reciprocal(out=out, in_=in_), reads=reads, writes=writes)

        def bsl(b):
            return slice(b * BLK, (b + 1) * BLK)

        S.add("sp", lambda e, sem: e.dma_start(out=cst[:], in_=cst_d).then_inc(sem, 16), writes=[cstR], dma=True)
        S.add("sp", lambda e, sem: e.dma_start(out=xT[:, 0, 0:NINIT], in_=cin_d).then_inc(sem, 16), writes=xR[0], dma=True, dres=xR[0][0])
        for i, off in enumerate((I_ONES, I_IDENT, I_PERM, I_NEG)):
            cp(cbf[:, i, :], xT[:, 0, off:off + 128], xR[0], [cbfR])
        cp(qdb[:].rearrange("p h m -> p (h m)"), xT[:, 0, I_QD:I_QD + 768], xR[0], [cbfR])

        W = WRing()

        def rstd_from(ss_ap, ss_res, n, width=BLK):
            rs, rsR = rsdr.next()
            act(rs[:, 0:width], ss_ap, AF.Ln, [ss_res, cstR], [rsR], bias=eps_c, scale=1.0 / n)
            act(rs[:, 0:width], rs[:, 0:width], AF.Exp, [rsR], [rsR], scale=-0.5)
            return rs, rsR

        NBring = Ring([(pbank[6], bankR[6]), (pbank[7], bankR[7])])
        prenormed = set()
        sqRing = Ring([(scrF[i][:].bitcast(BF)[:, 0:BLK], scrFr.items[i][1]) for i in range(3)])

        def norm_groups(l, gi, b, db):
            st = {}

            def g_sq(c0):
                def f():
                    if c0 == 0:
                        if b >= 2:
                            flush()
                        st["ss"] = NBring.next()
                    ss, ssR = st["ss"]
                    sqs = []
                    for c in (c0, c0 + 1):
                        sq, sqR = sqRing.next()
                        act(sq, xT[:, c, bsl(b)], AF.Square, [xR[c][b]], [sqR])
                        sqs.append((c, sq, sqR))
                    for c, sq, sqR in sqs:
                        mm(ss[:], ones_bf, sq, c == 0, c == NCH - 1, [sqR, cbfR], [ssR])
                    if c0 + 2 == NCH:
                        st["rs"] = rstd_from(ss[:], ssR, D)
                return f

            def g_out(c0):
                def f():
                    rs, rsR = st["rs"]
                    for c in (c0, c0 + 1):
                        stt(hT[:, c, bsl(db)], xT[:, c, bsl(b)], gcol(gi, l, c), rs[:], ALU.mult, ALU.mult,
                            [xR[c][b], rsR, cstR], [hR[c][db]])
                return f

            return [g_sq(c0) for c0 in range(0, NCH, 2)] + [g_out(c0) for c0 in range(0, NCH, 2)]

        def norm_x(l, gi, blks, dst_blk0=0):
            for b in blks:
                if (l, gi, b) in prenormed:
                    prenormed.discard((l, gi, b))
                    continue
                for g in norm_groups(l, gi, b, b - dst_blk0):
                    g()
                drain(4)

        def proj_unit(key, src, srcR, blks, ring):
            wu, wR = W.get(key)
            outs = []
            for b in blks:
                pb, pR = ring.next()
                for k in range(NCH):
                    mm(pb[:], wu[:, k, :], src[:, k, bsl(b)], k == 0, k == NCH - 1, [wR, srcR[k][b]], [pR])
                outs.append((pb, pR, b))
                yield pb, pR, b
            W.done()

        qR = [Res(f"q{b}") for b in range(NBLK)]
        kR = [Res(f"k{b}") for b in range(NBLK)]
        memhR = [[qR[c // 2]] for c in range(NCH)]
        memT = u2hi[:, 12288:16384].bitcast(F32).rearrange("p (c m) -> p c m", c=NCH)
        mqT = u2hi[:, 8192:12288].rearrange("p (u t) -> p u t", u=2)
        memR = Res("memT")
        mqR = [[Res(f"mq{u}_{b}") for b in range(NBLK)] for u in range(2)]
        memhT = u2hi[:, 0:2048].rearrange("p (c m) -> p c m", c=NCH)
        mkT = sb("mkT_s", [128, 2, MEM], BF)
        mkR = [Res("mk0"), Res("mk1")]
        mvaug = sb("mvaug_s", [128, 2, 4, 128], BF)
        mvR = Res("mvaug")
        PTr = scrBr
        rl = sb("rl_s", [128, BLK], F32)
        rlR = Res("rl")

        S.add("dve", lambda e: e.memset(mvaug[:].rearrange("p a b c -> p (a b c)"), 1.0), writes=[mvR])

        def mem_kv(s, l):
            S.add("sp", lambda e, sem: [e.dma_start(out=memT[:, c, :], in_=memT_d[s, c]).then_inc(sem, 16) for c in range(NCH)],
                  reads=[TM], writes=[memR], dma=True, ndma=NCH)
            ss, ssR = Aring.next()
            for c in range(NCH):
                sq, sqR = scrBr.next()
                act(sq[:, 0:MEM], memT[:, c, :], AF.Square, [memR, TM], [sqR])
                mm(ss[:, 0:MEM], ones_bf, sq[:, 0:MEM], c == 0, c == NCH - 1, [sqR, cbfR], [ssR])
            rs, rsR = rstd_from(ss[:, 0:MEM], ssR, D, MEM)
            for c in range(NCH):
                stt(memhT[:, c, :], memT[:, c, :], gcol(4, l, c), rs[:, 0:MEM], ALU.mult, ALU.mult,
                    [memR, rsR, cstR, TM], [memhR[c][0]])
            for u in range(2):
                wu, wR = W.get(("mk", l, u))
                pb, pR = Pring.next()
                for k in range(NCH):
                    mm(pb[:, 0:MEM], wu[:, k, :], memhT[:, k, :], k == 0, k == NCH - 1, [wR, memhR[k][0]], [pR])
                act(mkT[:, u, :], pb[:, 0:MEM], AF.Copy, [pR], [mkR[u]])
                W.done()
            for u in range(2):
                wu, wR = W.get(("mv", l, u))
                for kt in range(2):
                    sp_, spR = smRing.next()
                    for k in range(NCH):
                        mm(sp_, memhT[:, k, kt * 128:(kt + 1) * 128], wu[:, k, :], k == 0, k == NCH - 1,
                           [wR, memhR[k][0]], [spR])
                    act(mvaug[:, kt, 2 * u, 0:64], sp_[:, 0:64], AF.Copy, [spR], [mvR])
                    cp(mvaug[:, kt, 2 * u + 1, 64:128], sp_[:, 64:128], [spR], [mvR])
                W.done()

        def mem_qattn(s, l):
            for u in range(2):
                for pb, pR, b in proj_unit(("mq", l, u), hT, hR, range(NBLK), Pring):
                    act(mqT[:, u, bsl(b)], pb[:], AF.Copy, [pR, TM], [mqR[u][b]])
            obs = {}
            pend = []

            def emit_pv(u, hh, qb, kt, pt, ptR):
                ob, oR = obs[(u, hh, qb)]
                mm(ob[:], mvaug[:, kt, 2 * u + hh, :], pt[:], kt == 0, kt == 1, [mvR, ptR], [oR])
                if kt == 1:
                    finish_attn(ob, oR, hh, 6 + u, qb)

            for u in range(2):
                for hh in range(2):
                    p0 = 64 * hh
                    for qb in range(NBLK):
                        for kt in range(2):
                            if kt == 0:
                                obs[(u, hh, qb)] = Oring.next()
                            sb_, sR = P4ring.next()
                            mm(sb_[:], mkT[p0:p0 + 64, u, kt * 128:(kt + 1) * 128], mqT[p0:p0 + 64, u, bsl(qb)], True, True,
                               [mkR[u], mqR[u][qb], TM], [sR])
                            pt, ptR = PTr.next()
                            act(pt[:], sb_[:], AF.Exp, [sR], [ptR], scale=0.125)
                            pend.append((u, hh, qb, kt, pt, ptR))
                            if len(pend) > 2:
                                emit_pv(*pend.pop(0))
            while pend:
                emit_pv(*pend.pop(0))

        def finish_attn(ob, oR, hh, chunk, qb):
            p0 = 64 * hh
            q0 = 64 - p0
            act(rl[p0:p0 + 64, :], ob[q0:q0 + 64, :], AF.Ln, [oR], [rlR])
            act(rl[p0:p0 + 64, :], rl[p0:p0 + 64, :], AF.Exp, [rlR], [rlR], scale=-1.0)
            tt(mixT[p0:p0 + 64, chunk, bsl(qb)], ob[p0:p0 + 64, :], rl[p0:p0 + 64, :], ALU.mult, [oR, rlR], [mR[chunk][qb]])

        cosT = u2hi[:, 8192:12288].bitcast(F32)
        sinT = u2hi[:, 12288:16384].bitcast(F32)
        tabC = [Res(f"tabC{b}") for b in range(NBLK)]
        tabS = [Res(f"tabS{b}") for b in range(NBLK)]
        perm_f = cst[:, C_PERM:C_PERM + 128]
        dbc = tab[:].bitcast(F32)
        posi = rl[:].bitcast(I32)
        posiR = rlR
        TWO_PI = 2.0 * math.pi
        CW1 = float(np.float32(6.28125))
        CW2 = float(np.float32(TWO_PI - 6.28125))

        def rotary_tables(s):
            for b in range(NBLK):
                S.add("sp", (lambda b_: lambda e, sem: e.dma_start(out=posi[:], in_=pos_d[s:s + 1, bsl(b_)].broadcast_to([128, BLK])).then_inc(sem, 16))(b),
                      writes=[posiR], dma=True)
                ang, angR = scrFr.next()
                cp(ang[:], posi[:], [posiR], [angR])
                ts(ang[:], ang[:], cst[:, C_INVF:C_INVF + 1], None, ALU.mult, None, [angR, cstR], [angR])
                kf, kfR = scrFr.next()
                ts(kf[:], ang[:], 1.0 / TWO_PI, None, ALU.mult, None, [angR], [kfR])
                cp(posi[:], kf[:], [kfR], [posiR])
                cp(kf[:], posi[:], [posiR], [kfR])
                stt(ang[:], kf[:], -CW1, ang[:], ALU.mult, ALU.add, [kfR, angR], [angR])
                stt(ang[:], kf[:], -CW2, ang[:], ALU.mult, ALU.add, [kfR, angR], [angR])
                for which, shift in ((0, 0.0), (1, math.pi / 2)):
                    y, yR = scrFr.next()
                    ts(y[:], ang[:], shift, None, ALU.add, None, [angR], [yR])
                    ts(kf[:], y[:], math.pi, -TWO_PI, ALU.is_gt, ALU.mult, [yR], [kfR])
                    tt(y[:], y[:], kf[:], ALU.add, [yR, kfR], [yR])
                    ts(kf[:], y[:], -math.pi, TWO_PI, ALU.is_lt, ALU.mult, [yR], [kfR])
                    tt(y[:], y[:], kf[:], ALU.add, [yR, kfR], [yR])
                    ts(y[:], y[:], 3.1415925, -3.1415925, ALU.min, ALU.max, [yR], [yR])
                    if which == 0:
                        act(sinT[:, bsl(b)], y[:], AF.Sin, [yR, TM], [tabS[b], memR])
                    else:
                        act(cosT[:, bsl(b)], y[:], AF.Sin, [yR, TM], [tabC[b], mqR[b // 2][2 * (b % 2)], mqR[b // 2][2 * (b % 2) + 1]])

        qTp = u2hi[:, 0:2048]
        kTr = u2hi[:, 2048:4096]
        ktok = u2hi[:, 4096:6144].rearrange("p (n c) -> p n c", c=128)
        vtok = u2hi[:, 6144:8192].rearrange("p (n c) -> p n c", c=128)
        ktokR = [Res(f"ktok{n}") for n in range(NT)]
        vtokR = [Res(f"vtok{n}") for n in range(NT)]
        AT = [smallb[:, i * 128:(i + 1) * 128] for i in range(2)]
        ATr = Ring([(AT[i], Res(f"AT{i}")) for i in range(2)])
        stbf = [smallb[:, 256 + i * 128:256 + (i + 1) * 128] for i in range(2)]
        stbfr = Ring([(stbf[i], Res(f"stbf{i}")) for i in range(2)])
        state = small[:, 0:128]
        stateR = Res("state")
        sgs = [(sb("sg0_s", [128, BLK], BF), Res("sg0")), (rl[:].bitcast(BF)[:, 0:BLK], rlR)]

        qf0 = small[:, 1024:1152]
        kf0 = small[:, 1152:1280]
        qf0R = Res("qf0")
        kf0R = Res("kf0")

        def rope_unit(key, l, dstT, dstR, decay_h):
            pend = None

            def tail(b, raw, rawR, t1, t1R):
                pp, ppR = Aring.next()
                mm(pp[:], perm_f, raw[:], True, True, [rawR, cstR], [ppR])
                t2, t2R = scrFr.next()
                tt(t2[:], pp[:], sinT[:, bsl(b)], ALU.mult, [ppR, tabS[b], TM], [t2R])
                if decay_h is None:
                    tt(dstT[:, bsl(b)], t1[:], t2[:], ALU.add, [t1R, t2R, TM], [dstR[b]])
                    if b == 0:
                        tt(kf0, t1[:, 0:128], t2[:, 0:128], ALU.add, [t1R, t2R], [kf0R], eng="pool")
                else:
                    tt(t1[:], t1[:], t2[:], ALU.add, [t1R, t2R], [t1R])
                    qd = qdb[:, decay_h:decay_h + 1, :].broadcast_to([128, 4, 128])
                    tt(dstT[:, bsl(b)].rearrange("p (n c) -> p n c", c=128), t1[:].rearrange("p (n c) -> p n c", c=128), qd,
                       ALU.mult, [t1R, cbfR, TM], [dstR[b]], eng="pool")
                    if b == 0:
                        tt(qf0, t1[:, 0:128], qdb[:, decay_h, :], ALU.mult, [t1R, cbfR], [qf0R], eng="pool")

            for pb, pR, b in proj_unit(key, hT, hR, range(NBLK), Pring):
                if pend is not None:
                    tail(*pend)
                raw, rawR = scrFr.next()
                act(raw[:], pb[:], AF.Copy, [pR], [rawR])
                t1, t1R = scrFr.next()
                tt(t1[:], pb[:], cosT[:, bsl(b)], ALU.mult, [pR, tabC[b], TM], [t1R])
                pend = (b, raw, rawR, t1, t1R)
            tail(*pend)

        def retention_layer(s, l):
            for h in range(6):
                gam = 1.0 - 2.0 ** (-5.0 - h)
                cd = gam ** 128
                rope_unit(("q", l, h), l, qTp, qR, h)
                rope_unit(("k", l, h), l, kTr, kR, None)
                for n in range(NT):
                    tp, tpR = trRing.next()
                    S.add("pe", (lambda tp_, n_: lambda e: e.transpose(out=tp_, in_=kTr[:, n_ * 128:(n_ + 1) * 128], identity=ident_bf))(tp, n),
                          reads=[kR[n // 4], cbfR, TM], writes=[tpR])
                    act(ktok[:, n, :], tp, AF.Copy, [tpR, cstR, TM], [ktokR[n]], scale=cst[:, C_KD + h:C_KD + h + 1])
                wu, wR = W.get(("v", l, h))
                for n in range(NT):
                    sp_, spR = smRing.next()
                    for k in range(NCH):
                        mm(sp_, hT[:, k, n * 128:(n + 1) * 128], wu[:, k, :], k == 0, k == NCH - 1, [wR, hR[k][n // 4]], [spR])
                    if n % 2 == 0:
                        act(vtok[:, n, :], sp_, AF.Copy, [spR, TM], [vtokR[n]])
                    else:
                        cp(vtok[:, n, :], sp_, [spR, TM], [vtokR[n]])
                W.done()
                gu, gR = W.get(("g", l, h))
                obs = {}
                sb_for = {}
                mask_h = cst[:, C_MASK + h * 128:C_MASK + (h + 1) * 128]

                def emit_o(n, at, atR):
                    b = n // 4
                    ob, oR = obs[b]
                    csl = slice((n % 4) * 128, (n % 4 + 1) * 128)
                    tsl = slice(n * 128, (n + 1) * 128)
                    mm(ob[:, csl], vtok[:, n, :], at, True, n == 0, [vtokR[n], atR, TM], [oR])
                    if n > 0:
                        sbf, sbfR = sb_for[n]
                        mm(ob[:, csl], sbf, qTp[:, tsl], False, True, [sbfR, qR[b], TM], [oR])
                    if n % 4 == 3:
                        rsq, rsqR = scrBr.next()
                        act(rsq[:], ob[:], AF.Square, [oR], [rsqR])
                        gnext = gate_pe(b + 1) if b + 1 < NBLK else None
                        ss, ssR = Aring.next()
                        mm(ss[:], ones_bf, rsq[:], True, True, [rsqR, cbfR], [ssR])
                        rs, rsR = rstd_from(ss[:], ssR, 128)
                        if gnext is not None:
                            gate_act(b + 1, *gnext)

                        def fin(b=b, ob=ob, oR=oR, rs=rs, rsR=rsR):
                            tmp, tmpR = scrFr.next()
                            tt(tmp[:], ob[:], rs[:], ALU.mult, [oR, rsR], [tmpR])
                            sgb, sgbR = sgs[b % 2]
                            tt(mixT[:, h, bsl(b)], tmp[:], sgb[:], ALU.mult, [tmpR, sgbR], [mR[h][b]], eng="pool")
                        late.append((n + 3, fin))

                def gate_pe(b):
                    gp, gpR = Pring.next()
                    for k in range(NCH):
                        mm(gp[:], gu[:, k, :], hT[:, k, bsl(b)], k == 0, k == NCH - 1, [gR, hR[k][b]], [gpR])
                    return gp, gpR

                def gate_act(b, gp, gpR):
                    e1, e1R = scrFr.next()
                    act(e1[:], gp[:], AF.Exp, [gpR], [e1R], scale=-1.0)
                    act(e1[:], e1[:], AF.Ln, [e1R, cstR], [e1R], bias=one_c, scale=1.0)
                    act(e1[:], e1[:], AF.Exp, [e1R], [e1R], scale=-1.0)
                    def fin(b=b, gp=gp, gpR=gpR, e1=e1, e1R=e1R):
                        sgb, sgbR = sgs[b % 2]
                        tt(sgb[:], gp[:], e1[:], ALU.mult, [gpR, e1R], [sgbR])
                    late.append((4 * b + 2, fin))

                late = []
                pend_o = None
                for n in range(NT):
                    b = n // 4
                    if n % 4 == 0:
                        obs[b] = Oring.next()
                        if b == 0:
                            gate_act(0, *gate_pe(0))
                    tsl = slice(n * 128, (n + 1) * 128)
                    sp_, spR = smRing.next()
                    if n == 0:
                        mm(sp_, kf0, qf0, True, True, [kf0R, qf0R], [spR])
                    else:
                        mm(sp_, kTr[:, tsl], qTp[:, tsl], True, True, [kR[b], qR[b], TM], [spR])
                    if n < NT - 1:
                        kv, kvR = smRing.next()
                        mm(kv, ktok[:, n, :], vtok[:, n, :], True, True, [ktokR[n], vtokR[n], TM], [kvR])
                    at, atR = ATr.next()
                    tt(at, sp_, mask_h, ALU.mult, [spR, cstR], [atR])
                    if pend_o is not None:
                        emit_o(*pend_o)
                    pend_o = (n, at, atR)
                    if n < NT - 1:
                        if n == 0:
                            cp(state, kv, [kvR], [stateR])
                        else:
                            stt(state, state, cd, kv, ALU.mult, ALU.add, [stateR, kvR], [stateR])
                        sbf, sbfR = stbfr.next()
                        cp(sbf, state, [stateR], [sbfR])
                        sb_for[n + 1] = (sbf, sbfR)
                    while late and late[0][0] <= n:
                        late.pop(0)[1]()
                emit_o(*pend_o)
                while late:
                    late.pop(0)[1]()
                W.done()

        vaug = u2hi[:, 4096:8192].rearrange("p (t h e) -> p t h e", t=NT, h=2)
        vaugR = [Res(f"vaug{n}") for n in range(NT)]
        nlogf = small[:, 128:128 + 192].rearrange("p (t h) -> p t h", h=12)
        carry = small[:, 320:320 + 192].rearrange("p (t h) -> p t h", h=12)
        dcol = small[:, 512:512 + 192].rearrange("p (t h) -> p t h", h=12)
        sprev = small[:, 704:704 + 192].rearrange("p (t h) -> p t h", h=12)
        zt = small[:, 960:972]
        ztR = Res("zt")
        nlogfR = Res("nlogf")
        carryR = Res("carry")
        dcolR = Res("dcol")
        sprevR = Res("sprev")
        drow = tab[0:12, 0:2048]
        drowR = Res("drow")
        dqsem = [Res("dqA"), Res("dqB")]
        qX = [u2hi[:, 0:2048], u2hi[:, 8192:10240]]
        kX = [u2hi[:, 2048:4096], u2hi[:, 10240:12288]]
        qXR = [qR, mqR[0]]
        kXR = [kR, mqR[1]]

        def fox_layer(s, l):
            S.add("dve", lambda e: e.memset(u2hi[:, 4096:8192], 1.0), reads=[TM], writes=vaugR)
            for hh in range(2):
                S.add("dve", (lambda hh_: lambda e: e.memset(kX[hh_][64:128, :], 0.0))(hh), reads=[TM], writes=kXR[hh])
                S.add("dve", (lambda hh_: lambda e: e.memset(kX[hh_][64:65, :], 1.0))(hh), reads=[TM], writes=kXR[hh])
                S.add("dve", (lambda hh_: lambda e: e.memset(qX[hh_][64:128, :], 0.0))(hh), reads=[TM], writes=qXR[hh])
            nl_flat = small[:, 128:320]
            ca_flat = small[:, 320:512]
            dc_flat = small[:, 512:704]
            fb, fbR = Aring.next()
            wu, wR = W.get(("f", l, 0))
            fl = fb[:, 0:192].rearrange("p (t h) -> p t h", h=12)
            for n in range(NT):
                for k in range(NCH):
                    mm(fl[:, n, :], hT[:, k, n * 128:(n + 1) * 128], wu[:, k, 0:12], k == 0, k == NCH - 1,
                       [wR, hR[k][n // 4]], [fbR])
            W.done()
            bfb = cst[:, C_BF:C_BF + 12].rearrange("p (o h) -> p o h", o=1).broadcast_to([128, NT, 12])
            tt(nlogf, fl, bfb, ALU.add, [fbR, cstR], [nlogfR])
            act(nl_flat, nl_flat, AF.Exp, [nlogfR], [nlogfR], scale=-1.0)
            act(nl_flat, nl_flat, AF.Ln, [nlogfR, cstR], [nlogfR], bias=one_c, scale=1.0)
            tb, tbR = Aring.next()
            mm(tb[:, 0:192], ones_f, nl_flat, True, True, [cstR, nlogfR], [tbR])
            db, dbR = Aring.next()
            mm(db[:, 0:192], U_f, nl_flat, True, True, [cstR, nlogfR], [dbR])
            S.add("dve", lambda e: e.memset(carry[:, 0, :], 0.0), writes=[carryR])
            S.add("dve", lambda e: e.memset(sprev[:, 0, :], 0.0), writes=[sprevR])
            for n in range(NT - 1):
                tt(carry[:, n + 1, :], carry[:, n, :], tb[:, n * 12:(n + 1) * 12], ALU.add, [tbR, carryR], [carryR])
                tt(sprev[:, n + 1, :], sprev[:, n, :], nlogf[:, n, :], ALU.add, [sprevR, nlogfR], [sprevR], eng="pool")
            tt(dc_flat, db[:, 0:192], ca_flat, ALU.add, [dbR, carryR], [dcolR])
            for g in range(NBLK):
                rb, rbR = P4ring.next()
                for t in range(4):
                    n = 4 * g + t
                    mm(rb[0:12, t * 128:(t + 1) * 128], nlogf[:, n, :], U_f, True, n == 0, [cstR, nlogfR], [rbR])
                    if n > 0:
                        mm(rb[0:12, t * 128:(t + 1) * 128], sprev[:, n, :], ones_f, False, True, [cstR, sprevR], [rbR])
                act(drow[:, bsl(g)], rb[0:12, :], AF.Copy, [rbR], [drowR], scale=-8.0)
            for p in range(6):
                for hh in range(2):
                    S.add("sp", (lambda hh_, h_: lambda e, sem: e.dma_start(out=qX[hh_][64:65, :], in_=tab[h_:h_ + 1, 0:2048]).then_inc(sem, 16))(hh, 2 * p + hh),
                          reads=[drowR, TM], writes=qXR[hh], dma=True, dres=dqsem[hh])
                for pb, pR, b in proj_unit(("q", l, p), hT, hR, range(NBLK), Pring):
                    cp(qX[0][0:64, bsl(b)], pb[0:64, :], [pR, TM], [qXR[0][b]])
                    cp(qX[1][0:64, bsl(b)], pb[64:128, :], [pR, TM], [qXR[1][b]])
                for pb, pR, b in proj_unit(("k", l, p), hT, hR, range(NBLK), Pring):
                    cp(kX[0][0:64, bsl(b)], pb[0:64, :], [pR, TM], [kXR[0][b]])
                    cp(kX[1][0:64, bsl(b)], pb[64:128, :], [pR, TM], [kXR[1][b]])
                wu, wR = W.get(("v", l, p))
                for n in range(NT):
                    sp_, spR = smRing.next()
                    for k in range(NCH):
                        mm(sp_, hT[:, k, n * 128:(n + 1) * 128], wu[:, k, :], k == 0, k == NCH - 1, [wR, hR[k][n // 4]], [spR])
                    cp(vaug[:, n, 0, 0:64], sp_[:, 0:64], [spR, TM], [vaugR[n]])
                    cp(vaug[:, n, 1, 64:128], sp_[:, 64:128], [spR, TM], [vaugR[n]])
                W.done()
                steps = [(hh, qb, kt) for hh in range(2) for qb in range(NBLK) for kt in range(4 * qb + 4)]
                obs = {}
                pend = []

                def emit_pv(hh, qb, kt, pt, ptR, c0, w):
                    ob, oR = obs[(hh, qb)]
                    nkt = 4 * qb + 4
                    mm(ob[:, c0:BLK], vaug[:, kt, hh, :], pt[:, 0:w], kt == 0, kt == nkt - 1, [vaugR[kt], ptR, TM], [oR])
                    if kt == nkt - 1:
                        finish_attn(ob, oR, hh, p, qb)

                for (hh, qb, kt) in steps:
                    h = 2 * p + hh
                    if kt == 0:
                        obs[(hh, qb)] = Oring.next()
                    r = max(0, kt - 4 * qb)
                    c0 = 128 * r
                    w = BLK - c0
                    sb_, sR = P4ring.next()
                    mm(sb_[:, 0:w], kX[hh][:, kt * 128:(kt + 1) * 128], qX[hh][:, qb * BLK + c0:(qb + 1) * BLK],
                       True, kt < 4 * qb, [kXR[hh][kt // 4], qXR[hh][qb], TM], [sR])
                    if kt >= 4 * qb:
                        mm(sb_[:, 0:128], ident_bf, neg_bf, False, True, [cbfR], [sR])
                    pt, ptR = PTr.next()
                    act(pt[:, 0:w], sb_[:, 0:w], AF.Exp, [sR, dcolR], [ptR], bias=dcol[:, kt, h:h + 1], scale=0.125)
                    pend.append((hh, qb, kt, pt, ptR, c0, w))
                    if len(pend) > 2:
                        emit_pv(*pend.pop(0))
                while pend:
                    emit_pv(*pend.pop(0))

        deferred = []

        def drain(k=1):
            for _ in range(k):
                if deferred:
                    deferred.pop(0)()

        def flush():
            while deferred:
                deferred.pop(0)()

        def rstd_inplace(bank, bR, n):
            act(bank[:], bank[:], AF.Ln, [bR, cstR], [bR], bias=eps_c, scale=1.0 / n)
            act(bank[:], bank[:], AF.Exp, [bR], [bR], scale=-0.5)

        def tail_pair(stage, stageR, gc, bank, bR, xap, xRes, oc):
            def f():
                t, tR = scrFr.next()
                stt(t[:], stage, gc, bank[:], ALU.mult, ALU.mult, [stageR, bR, cstR], [tR])
                tt(xap, xap, t[:], ALU.add, [xRes, tR], [xRes], eng=("pool" if oc % 2 else "dve"))
            return f

        def out_proj(l):
            ssb = [(pbank[2 + b], bankR[2 + b]) for b in range(NBLK)]
            pend = None
            for oc in range(NCH):
                for pb, pR, b in proj_unit(("out", l, oc), mixT, mR, range(NBLK), Pring):
                    cp(hT[:, oc, bsl(b)], pb[:], [pR], [hR[oc][b]])
                    sq, sqR = scrBr.next()
                    act(sq[:], pb[:], AF.Square, [pR], [sqR])
                    if pend is not None:
                        mm(*pend)
                    pend = (ssb[b][0][:], ones_bf, sq[:], oc == 0, oc == NCH - 1, [sqR, cbfR], [ssb[b][1]])
            mm(*pend)
            for b in range(NBLK):
                rstd_inplace(ssb[b][0], ssb[b][1], D)
            for b in range(NBLK):
                for oc in range(NCH):
                    f = tail_pair(hT[:, oc, bsl(b)], hR[oc][b], gcol(1, l, oc), ssb[b][0], ssb[b][1], xT[:, oc, bsl(b)], xR[oc][b], oc)
                    if b < 2:
                        f()
                    else:
                        deferred.append(f)

        def u2T(f, hb):
            if f < 16:
                return mixT[:].rearrange("p c t -> p (c t)")[:, f * 1024 + hb * BLK:f * 1024 + (hb + 1) * BLK]
            return u2hi[:, (f - 16) * 1024 + hb * BLK:(f - 16) * 1024 + (hb + 1) * BLK]

        u2R = [[Res(f"u2_{f}_{hb}") for hb in range(2)] for f in range(32)]
        for f in range(16):
            for hb in range(2):
                u2R[f][hb] = mR[f // 2][(f % 2) * 2 + hb]

        def mlp(l, after_half0=None, next_layer=None):
            S.add("dve", lambda e: e.memset(junk[:], 0.0), writes=[TM, junkR])
            for half in range(2):
                blks = [2 * half, 2 * half + 1]
                norm_x(l, 2, blks, dst_blk0=2 * half)
                hRl = [[hR[c][0], hR[c][1]] for c in range(NCH)]
                for f in range(32):
                    tmr = [TM] if f >= 16 else []
                    for pb, pR, b in proj_unit(("up", l, f), hT, hRl, range(2), P4ring):
                        r, rR = scrBr.next()
                        act(r[:], pb[:], AF.Relu, [pR], [rR])
                        tt(u2T(f, b), r[:], r[:], ALU.mult, [rR] + tmr, [u2R[f][b]])
                    drain(1)
                flush()
                if half == 1 and after_half0 is not None:
                    after_half0()
                ssb = [(pbank[4 + b], bankR[4 + b]) for b in range(2)]
                pendq = []
                early = []
                if half == 0 and EARLY:
                    for b in (2, 3):
                        early += norm_groups(l, 2, b, b - 2)
                        prenormed.add((l, 2, b))
                elif next_layer is not None and EARLY:
                    for b in (0, 1):
                        early += norm_groups(next_layer, 0, b, b)
                        prenormed.add((next_layer, 0, b))
                for oc in range(NCH):
                    acc = [P4ring.next() for _ in range(2)]
                    for g in range(4):
                        if oc >= 1 and early:
                            early.pop(0)()
                        if g == 1:
                            while pendq:
                                mm(*pendq.pop(0))
                        wu, wR = W.get(("down", l, oc * 4 + g))
                        for k in range(NCH):
                            f = g * 8 + k
                            tmr = [TM] if f >= 16 else []
                            for b in range(2):
                                mm(acc[b][0][:], wu[:, k, :], u2T(f, b), f == 0, f == 31, [wR, u2R[f][b]] + tmr, [acc[b][1]])
                        W.done()
                    for b in range(2):
                        cp(hT[:, oc, bsl(2 + b)], acc[b][0][:], [acc[b][1]], [hR[oc][2 + b]])
                        sq, sqR = scrBr.next()
                        act(sq[:], acc[b][0][:], AF.Square, [acc[b][1]], [sqR])
                        pendq.append((ssb[b][0][:], ones_bf, sq[:], oc == 0, oc == NCH - 1, [sqR, cbfR], [ssb[b][1]]))
                while pendq:
                    mm(*pendq.pop(0))
                while early:
                    early.pop(0)()
                for b in range(2):
                    rstd_inplace(ssb[b][0], ssb[b][1], D)
                for b in range(2):
                    xb = 2 * half + b
                    for oc in range(NCH):
                        deferred.append(tail_pair(hT[:, oc, bsl(2 + b)], hR[oc][2 + b], gcol(3, l, oc), ssb[b][0], ssb[b][1],
                                                  xT[:, oc, bsl(xb)], xR[oc][xb], oc))
            S.add("dve", lambda e: e.memset(junk[:], 0.0), writes=[TM, junkR])

        xsem = [Res(f"xsem{b}") for b in range(NBLK)]

        def load_x(s, b):
            S.add("sp", lambda e, sem: e.dma_start(out=xT[:, :, bsl(b)], in_=xT_d[s][:, :, bsl(b)].rearrange("c p t -> p c t")).then_inc(sem, 16),
                  writes=[xR[c][b] for c in range(NCH)], dma=True, dres=xsem[b])

        def store_x(s, b):
            S.add("sp", lambda e, sem: e.dma_start(out=out_d[s][:, :, bsl(b)].rearrange("c p t -> p c t"), in_=xT[:, :, bsl(b)]).then_inc(sem, 16),
                  reads=[xR[c][b] for c in range(NCH)], dma=True, dres=xsem[b], final=True)

        xstate = {"early": None}

        def early_io(s):
            for b in range(2):
                store_x(s, b)
            if s + 1 < n_seq:
                for b in range(2):
                    load_x(s + 1, b)
            xstate["early"] = s
        for s in range(n_seq):
            for b in range(NBLK):
                if not (s > 0 and b < 2):
                    load_x(s, b)
            for l in layers:
                if stop >= 3:
                    mem_kv(s, l)
                if stop >= 2:
                    norm_x(l, 0, range(NBLK))
                if stop >= 3:
                    mem_qattn(s, l)
                if l % 2 == 0 and stop >= 1:
                    rotary_tables(s)
                if stop >= 4:
                    if l % 2 == 0:
                        retention_layer(s, l)
                    else:
                        fox_layer(s, l)
                if stop >= 5:
                    out_proj(l)
                if stop >= 6:
                    last = (l == layers[-1])
                    mlp(l, (lambda s_=s: early_io(s_)) if last else None, None if last else layers[layers.index(l) + 1])
            flush()
            for b in range(NBLK):
                if not (xstate["early"] == s and b < 2):
                    store_x(s, b)
        assert stop < 6 or W.consumed == len(plan), (W.consumed, len(plan))
        S.emit(st)
    return nc


_CACHE = {}


def kernel(**inp):
    inp = {k: np.asarray(v) for k, v in inp.items()}
    n_cores = 8
    n_seq = 2
    layers = (0, 1)
    wall, widx = pack_wall(inp, layers)
    cst, cin = build_consts(inp)
    key = ("prog", n_seq, layers, wall.shape[0])
    if key not in _CACHE:
        _CACHE[key] = build_program(n_seq, layers, widx, wall.shape[0])
    nc = _CACHE[key]
    x = inp["x"].astype(np.float32, copy=False)
    mem = inp["mem"].astype(np.float32, copy=False)
    pos = inp["positions"].astype(np.int32, copy=False)
    in_maps = []
    for c in range(n_cores):
        sl = slice(c * n_seq, (c + 1) * n_seq)
        xT = np.ascontiguousarray(x[sl].transpose(0, 2, 1)).reshape(n_seq, NCH, 128, SEQ)
        mT = np.ascontiguousarray(mem[sl].transpose(0, 2, 1)).reshape(n_seq, NCH, 128, MEM)
        in_maps.append({"xT": xT, "memT": mT, "pos": np.ascontiguousarray(pos[sl]), "wall": wall, "cst": cst, "cin": cin})
    res = run_bass_kernel_spmd(nc, in_maps, core_ids=list(range(n_cores)))
    outs = []
    for c in range(n_cores):
        o = np.asarray(res.results[c]["outT"]).reshape(n_seq, D, SEQ).transpose(0, 2, 1)
        outs.append(o)
    return np.ascontiguousarray(np.concatenate(outs, axis=0)).astype(np.float32, copy=False)
```
